# Optimizing a Trainium2 kernel written in Bass

```python
import jax, jax.numpy as jnp
from jax import lax
import numpy as np

D_MODEL = 2048
BATCH = 32
SEQ = 256
DEPTH = 4
DEC_BATCH = 4
DEC_SEQ = 2048
PAST_LEN = 256

GRID_W = 64
HEAD_DIM = 128
N_HEADS_A = 8
N_KV_A = 2
N_HEADS_B = 8
NOPE_B = 128
ROPE_B = 64
VDIM_B = 128
KV_RANK = 256
D_FF = ((8 * D_MODEL + 3 * 256 - 1) // (3 * 256)) * 256
Q_BLOCK = 128
ROPE_THETA = 10000.0
EPS = 1e-6

W_QA = N_HEADS_A * HEAD_DIM
W_KA = N_KV_A * HEAD_DIM
W_QB = N_HEADS_B * (NOPE_B + ROPE_B)
IN_COLS = W_QA + 2 * W_KA + W_QB + KV_RANK + ROPE_B
D_MIX = N_HEADS_A * HEAD_DIM + N_HEADS_B * VDIM_B

kernel_name = "hybrid_gqa_mla_diffusion_step"


def rmsnorm(x, g):
    xf = x.astype(jnp.float32)
    y = xf * lax.rsqrt(jnp.mean(xf * xf, axis=-1, keepdims=True) + EPS)
    return (y * g.astype(jnp.float32)).astype(x.dtype)


def grid_angles(n_tokens, dim):
    n_rows = n_tokens // GRID_W
    row = jnp.repeat(jnp.arange(n_rows, dtype=jnp.float32), GRID_W)
    col = jnp.tile(jnp.arange(GRID_W, dtype=jnp.float32), n_rows)
    n_freq = dim // 4
    inv = ROPE_THETA ** (-jnp.arange(n_freq, dtype=jnp.float32) / n_freq)
    ang = jnp.concatenate([row[:, None] * inv, col[:, None] * inv], axis=-1)
    return jnp.cos(ang), jnp.sin(ang)


def apply_rope(x, cos, sin):
    xf = x.astype(jnp.float32)
    half = x.shape[-1] // 2
    x1, x2 = xf[..., :half], xf[..., half:]
    cs, sn = cos[None, :, None, :], sin[None, :, None, :]
    return jnp.concatenate([x1 * cs - x2 * sn, x2 * cs + x1 * sn], axis=-1).astype(x.dtype)


def attention(q, k, v):
    B, Lq, H, dq = q.shape
    Hk, dv = k.shape[2], v.shape[-1]
    G = H // Hk
    nb = Lq // Q_BLOCK
    qb = q.reshape(B, nb, Q_BLOCK, Hk, G, dq).transpose(1, 0, 2, 3, 4, 5)
    kf = k.astype(jnp.float32)
    vf = v.astype(jnp.float32)
    scale = dq ** -0.5

    def one_block(qblk):
        s = jnp.einsum('bqhgd,bkhd->bhgqk', qblk.astype(jnp.float32), kf) * scale
        p = jax.nn.softmax(s, axis=-1)
        return jnp.einsum('bhgqk,bkhe->bqhge', p, vf)

    out = lax.map(one_block, qb)
    return out.transpose(1, 0, 2, 3, 4, 5).reshape(B, Lq, H, dv).astype(q.dtype)


def project(h, w_in_l, qnorm_l, knorm_l, kvnorm_l):
    B, L, _ = h.shape
    p = jnp.einsum('bld,de->ble', h, w_in_l)
    o1 = W_QA
    o2 = o1 + W_KA
    o3 = o2 + W_KA
    o4 = o3 + W_QB
    o5 = o4 + KV_RANK
    q_a = rmsnorm(p[..., :o1].reshape(B, L, N_HEADS_A, HEAD_DIM), qnorm_l)
    k_a = rmsnorm(p[..., o1:o2].reshape(B, L, N_KV_A, HEAD_DIM), knorm_l)
    v_a = p[..., o2:o3].reshape(B, L, N_KV_A, HEAD_DIM)
    q_b = p[..., o3:o4].reshape(B, L, N_HEADS_B, NOPE_B + ROPE_B)
    ckv = rmsnorm(p[..., o4:o5], kvnorm_l)
    krope = p[..., o5:]
    return q_a, k_a, v_a, q_b, ckv, krope


def mla_expand(ckv, krope, w_uk_l, w_uv_l):
    B, L, _ = ckv.shape
    k_nope = jnp.einsum('blr,rn->bln', ckv, w_uk_l).reshape(B, L, N_HEADS_B, NOPE_B)
    v = jnp.einsum('blr,rn->bln', ckv, w_uv_l).reshape(B, L, N_HEADS_B, VDIM_B)
    k_r = jnp.broadcast_to(krope[:, :, None, :], (B, L, N_HEADS_B, ROPE_B))
    return jnp.concatenate([k_nope, k_r], axis=-1), v


def mix_out(o_a, o_b, w_o_l):
    B, L = o_a.shape[:2]
    o = jnp.concatenate([o_a.reshape(B, L, -1), o_b.reshape(B, L, -1)], axis=-1)
    return jnp.einsum('ble,ed->bld', o, w_o_l)


def swiglu(h, wg, wu, wd):
    a = jnp.einsum('bld,df->blf', h, wg)
    b = jnp.einsum('bld,df->blf', h, wu)
    return jnp.einsum('blf,fd->bld', jax.nn.silu(a) * b, wd)


def modulate(x, shift, scale):
    return x * (1 + scale) + shift


def setup_inputs(seed: int = 0) -> dict:
    key = jax.random.key(seed)
    ks = jax.random.split(key, 24)
    f32 = jnp.float32
    nrm = lambda k, shape, s: jax.random.normal(k, shape, f32) * s
    return {
        "x_prompt": nrm(ks[0], (BATCH, SEQ, D_MODEL), 1.0),
        "x_sample": nrm(ks[1], (DEC_BATCH, DEC_SEQ, D_MODEL), 1.0),
        "cache_k_a": nrm(ks[2], (DEC_BATCH, DEPTH, PAST_LEN, N_KV_A, HEAD_DIM), 1.0),
        "cache_v_a": nrm(ks[3], (DEC_BATCH, DEPTH, PAST_LEN, N_KV_A, HEAD_DIM), 1.0),
        "cache_ckv_b": nrm(ks[4], (DEC_BATCH, DEPTH, PAST_LEN, KV_RANK), 1.0),
        "cache_krope_b": nrm(ks[5], (DEC_BATCH, DEPTH, PAST_LEN, ROPE_B), 1.0),
        "c": nrm(ks[6], (DEC_BATCH, D_MODEL), 1.0),
        "c_ctx": nrm(ks[7], (D_MODEL,), 1.0),
        "w_ada": nrm(ks[8], (DEPTH, D_MODEL, 6 * D_MODEL), 0.5 * D_MODEL ** -0.5),
        "b_ada": nrm(ks[9], (DEPTH, 6 * D_MODEL), 0.02),
        "norm_attn": 1.0 + nrm(ks[10], (DEPTH, D_MODEL), 0.02),
        "norm_ffn": 1.0 + nrm(ks[11], (DEPTH, D_MODEL), 0.02),
        "w_in": nrm(ks[12], (DEPTH, D_MODEL, IN_COLS), D_MODEL ** -0.5),
        "qnorm_a": 1.0 + nrm(ks[13], (DEPTH, HEAD_DIM), 0.02),
        "knorm_a": 1.0 + nrm(ks[14], (DEPTH, HEAD_DIM), 0.02),
        "kvnorm_b": 1.0 + nrm(ks[15], (DEPTH, KV_RANK), 0.02),
        "w_uk_b": nrm(ks[16], (DEPTH, KV_RANK, N_HEADS_B * NOPE_B), KV_RANK ** -0.5),
        "w_uv_b": nrm(ks[17], (DEPTH, KV_RANK, N_HEADS_B * VDIM_B), KV_RANK ** -0.5),
        "w_o": nrm(ks[18], (DEPTH, D_MIX, D_MODEL), D_MIX ** -0.5),
        "w_gate": nrm(ks[19], (DEPTH, D_MODEL, D_FF), D_MODEL ** -0.5),
        "w_up": nrm(ks[20], (DEPTH, D_MODEL, D_FF), D_MODEL ** -0.5),
        "w_down": nrm(ks[21], (DEPTH, D_FF, D_MODEL), D_FF ** -0.5),
        "norm_final": 1.0 + nrm(ks[22], (D_MODEL,), 0.02),
    }


def reference(x_prompt, x_sample, cache_k_a, cache_v_a, cache_ckv_b, cache_krope_b, c,
              c_ctx, w_ada, b_ada, norm_attn, norm_ffn, w_in, qnorm_a, knorm_a, kvnorm_b,
              w_uk_b, w_uv_b, w_o, w_gate, w_up, w_down, norm_final):
    xp = x_prompt
    xs = x_sample
    cos_a, sin_a = grid_angles(xs.shape[1], HEAD_DIM)
    cos_b, sin_b = grid_angles(xs.shape[1], ROPE_B)
    silu_ctx = jax.nn.silu(c_ctx)
    silu_c = jax.nn.silu(c)
    ks_a, vs_a, ckvs_b, kropes_b = [], [], [], []

    for l in range(DEPTH):
        mc = jnp.einsum('d,de->e', silu_ctx, w_ada[l]) + b_ada[l]
        sh1, sc1, g1, sh2, sc2, g2 = jnp.split(mc, 6, axis=-1)
        h = modulate(rmsnorm(xp, norm_attn[l]), sh1, sc1)
        q_a, k_a, v_a, q_b, ckv, krope = project(h, w_in[l], qnorm_a[l], knorm_a[l], kvnorm_b[l])
        k_b, v_b = mla_expand(ckv, krope, w_uk_b[l], w_uv_b[l])
        o_a = attention(q_a, k_a, v_a)
        o_b = attention(q_b, k_b, v_b)
        xp = xp + g1 * mix_out(o_a, o_b, w_o[l])
        h = modulate(rmsnorm(xp, norm_ffn[l]), sh2, sc2)
        xp = xp + g2 * swiglu(h, w_gate[l], w_up[l], w_down[l])
        ks_a.append(k_a)
        vs_a.append(v_a)
        ckvs_b.append(ckv)
        kropes_b.append(krope)

        ms = (jnp.einsum('bd,de->be', silu_c, w_ada[l]) + b_ada[l])[:, None, :]
        sh1, sc1, g1, sh2, sc2, g2 = jnp.split(ms, 6, axis=-1)
        h = modulate(rmsnorm(xs, norm_attn[l]), sh1, sc1)
        q_a, k_a, v_a, q_b, ckv, krope = project(h, w_in[l], qnorm_a[l], knorm_a[l], kvnorm_b[l])
        q_a = apply_rope(q_a, cos_a, sin_a)
        k_a = apply_rope(k_a, cos_a, sin_a)
        q_b = jnp.concatenate([q_b[..., :NOPE_B], apply_rope(q_b[..., NOPE_B:], cos_b, sin_b)], axis=-1)
        krope = apply_rope(krope[:, :, None, :], cos_b, sin_b)[:, :, 0, :]
        k_a_all = jnp.concatenate([cache_k_a[:, l], k_a], axis=1)
        v_a_all = jnp.concatenate([cache_v_a[:, l], v_a], axis=1)
        ckv_all = jnp.concatenate([cache_ckv_b[:, l], ckv], axis=1)
        krope_all = jnp.concatenate([cache_krope_b[:, l], krope], axis=1)
        k_b, v_b = mla_expand(ckv_all, krope_all, w_uk_b[l], w_uv_b[l])
        o_a = attention(q_a, k_a_all, v_a_all)
        o_b = attention(q_b, k_b, v_b)
        xs = xs + g1 * mix_out(o_a, o_b, w_o[l])
        h = modulate(rmsnorm(xs, norm_ffn[l]), sh2, sc2)
        xs = xs + g2 * swiglu(h, w_gate[l], w_up[l], w_down[l])

    y_prompt = rmsnorm(xp, norm_final)
    y_sample = rmsnorm(xs, norm_final)
    new_k_a = jnp.stack(ks_a, axis=1)
    new_v_a = jnp.stack(vs_a, axis=1)
    new_ckv_b = jnp.stack(ckvs_b, axis=1)
    new_krope_b = jnp.stack(kropes_b, axis=1)
    return (y_prompt, y_sample, new_k_a, new_v_a, new_ckv_b, new_krope_b)
```

```python
import numpy as np
import contextlib
import concourse.bass as bass
import concourse.mybir as mybir
from concourse.bass_utils import run_bass_kernel_spmd

F32 = mybir.dt.float32
BF16 = mybir.dt.bfloat16
AF = mybir.ActivationFunctionType
ALU = mybir.AluOpType
AX = mybir.AxisListType

D = 2048
T = 2048
L = 4
DFF = 5632
NKC = 16
NFC = 44
NKEY = 2304
NKB = 18
WIN = 3456
EPS = 1e-6
NEG = -30000.0

ENGS = ("pe", "act", "dve", "pool", "sp")
NDMASEM = 8


class Buf:
    __slots__ = ("name", "last_w", "readers", "dma_readers", "excl")

    def __init__(self, name):
        self.excl = False
        self.name = name
        self.last_w = None
        self.readers = {}
        self.dma_readers = []


class Op:
    __slots__ = ("eng", "fn", "deps", "is_dma", "sem", "val", "needs_sig")

    def __init__(self, eng, fn, is_dma):
        self.eng = eng
        self.fn = fn
        self.deps = set()
        self.is_dma = is_dma
        self.sem = None
        self.val = 0
        self.needs_sig = False


class Sched:
    def __init__(self):
        self.streams = {e: [] for e in ENGS}
        self.ndma = {e: 0 for e in ENGS}
        self.dma_hist = {e: [] for e in ENGS}
        self.bar_deps = {}

    def add(self, eng, fn, reads=(), writes=(), dma=False):
        op = Op(eng, fn, dma)
        deps = op.deps
        bd = self.bar_deps.get(eng)
        if bd:
            deps.update(bd)
            self.bar_deps[eng] = None
        for b in reads:
            if b.last_w is not None:
                deps.add(b.last_w)
            if b.excl:
                for e2, r in b.readers.items():
                    if e2 != eng:
                        deps.add(r)
        for b in writes:
            if b.last_w is not None:
                deps.add(b.last_w)
            for r in b.readers.values():
                deps.add(r)
            for r in b.dma_readers:
                deps.add(r)
        for b in reads:
            if dma:
                b.dma_readers.append(op)
            else:
                b.readers[eng] = op
        for b in writes:
            b.last_w = op
            b.readers = {}
            b.dma_readers = []
        deps.discard(op)
        if dma:
            j = self.ndma[eng]
            self.ndma[eng] = j + 1
            op.sem = ("dma", eng, j % NDMASEM)
            op.val = 16 * (j // NDMASEM + 1)
            hist = self.dma_hist[eng]
            if j >= NDMASEM:
                deps.add(hist[j - NDMASEM])
            hist.append(op)
        self.streams[eng].append(op)
        return op

    def barrier(self):
        deps = set()
        for e in ENGS:
            comp = [op for op in self.streams[e][-64:] if not op.is_dma]
            if comp:
                deps.add(comp[-1])
            else:
                comp = [op for op in self.streams[e] if not op.is_dma]
                if comp:
                    deps.add(comp[-1])
            for d in self.dma_hist[e][-NDMASEM:]:
                deps.add(d)
        for e in ENGS:
            self.bar_deps[e] = set(deps) | (self.bar_deps.get(e) or set())

    def plan(self):
        for e in ENGS:
            for op in self.streams[e]:
                for d in op.deps:
                    if not d.is_dma:
                        d.needs_sig = True
        for e in ENGS:
            cnt = 0
            for op in self.streams[e]:
                if op.is_dma:
                    continue
                if op.needs_sig:
                    cnt += 1
                    op.sem = ("prog", e)
                    op.val = cnt

    def emit(self, nc):
        self.plan()
        with contextlib.ExitStack() as es:
            sems = {}
            for e in ENGS:
                sems[("prog", e)] = es.enter_context(nc.semaphore("prog_" + e))
                for i in range(NDMASEM):
                    if self.ndma[e] > i:
                        sems[("dma", e, i)] = es.enter_context(nc.semaphore("dma_%s_%d" % (e, i)))
            block = es.enter_context(nc.Block())
            streams = self.streams
            dma_hist = self.dma_hist

            def run_stream(ename, eng):
                waited = {}
                for op in streams[ename]:
                    need = {}
                    for d in op.deps:
                        if need.get(d.sem, 0) < d.val:
                            need[d.sem] = d.val
                    for k, v in need.items():
                        if waited.get(k, 0) < v:
                            eng.wait_ge(sems[k], v)
                            waited[k] = v
                    ins = op.fn(eng)
                    if op.is_dma:
                        ins.then_inc(sems[op.sem], 16)
                    elif op.needs_sig:
                        ins.then_inc(sems[op.sem], 1)
                last = {}
                for op in dma_hist[ename]:
                    last[op.sem] = op.val
                for k, v in last.items():
                    if waited.get(k, 0) < v:
                        eng.wait_ge(sems[k], v)

            @block.tensor
            def _(eng):
                run_stream("pe", eng)

            @block.scalar
            def _(eng):
                run_stream("act", eng)

            @block.vector
            def _(eng):
                run_stream("dve", eng)

            @block.gpsimd
            def _(eng):
                run_stream("pool", eng)

            @block.sync
            def _(eng):
                run_stream("sp", eng)


class TB:
    __slots__ = ("t", "b")

    def __init__(self, t, name):
        self.t = t
        self.b = Buf(name)


_DT_SIZE = {F32: 4, BF16: 2}


class MK:
    def __init__(self, n_layers=L, stop=None, dbg=False):
        self.dbg = dbg
        self.stop = stop
        self.nl = n_layers
        self.nc = bass.Bass("TRN2", target_bir_lowering=False)
        self.S = Sched()
        self.off = 16512
        self.uid = 0

    def sb(self, name, shape, dt):
        n = 1
        for s in shape[1:]:
            n *= s
        nbytes = (n * _DT_SIZE[dt] + 63) // 64 * 64
        self.uid += 1
        t = self.nc.alloc_sbuf_tensor_at("%s_%d" % (name, self.uid), list(shape), dt, offset=self.off)
        self.off += nbytes
        assert self.off <= 229344, ("SBUF overflow", name, self.off)
        return TB(t, name)

    def din(self, name, shape, dt=F32):
        if self.dbg and name in ("w_ada", "w_gate", "w_up", "w_down", "w_o"):
            return self.nc.dram_tensor(name, [self.nl, 8, 8], dt, kind="ExternalInput").ap()
        return self.nc.dram_tensor(name, list(shape), dt, kind="ExternalInput").ap()

    def dout(self, name, shape, dt=F32):
        return self.nc.dram_tensor(name, list(shape), dt, kind="ExternalOutput").ap()

    def dma(self, q, out_ap, in_ap, reads=(), writes=()):
        self.S.add(q, lambda e: e.dma_start(out=out_ap, in_=in_ap), reads, writes, dma=True)

    def mm(self, items, reads, writes):
        def fn(e):
            ins = None
            for (o, l, r, st, sp) in items:
                ins = e.matmul(o, lhsT=l, rhs=r, start=st, stop=sp)
            return ins

        self.S.add("pe", fn, reads, writes)

    def mmg(self, out, pairs, reads, writes):
        n = len(pairs)
        self.mm([(out, l, r, i == 0, i == n - 1) for i, (l, r) in enumerate(pairs)], reads, writes)

    def tr(self, items, reads, writes):
        def fn(e):
            ins = None
            for (o, i, idn) in items:
                ins = e.transpose(o, i, idn)
            return ins

        self.S.add("pe", fn, reads, writes)

    def act(self, out, in_, func, R, W, scale=None, bias=None, accum=None):
        kw = {}
        if scale is not None:
            kw["scale"] = scale
        if bias is not None:
            kw["bias"] = bias
        if accum is not None:
            kw["accum_out"] = accum
        self.S.add("act", lambda e: e.activation(out=out, in_=in_, func=func, **kw), R, W)

    def cp(self, eng, out, in_, R, W):
        if eng == "act":
            self.S.add("act", lambda e: e.copy(out=out, in_=in_), R, W)
        else:
            self.S.add(eng, lambda e: e.tensor_copy(out=out, in_=in_), R, W)

    def tt(self, eng, out, in0, in1, op, R, W):
        self.S.add(eng, lambda e: e.tensor_tensor(out=out, in0=in0, in1=in1, op=op), R, W)

    def stt(self, eng, out, in0, scalar, in1, op0, op1, R, W):
        self.S.add(eng, lambda e: e.scalar_tensor_tensor(out=out, in0=in0, scalar=scalar, in1=in1, op0=op0, op1=op1), R, W)

    def ts1(self, eng, out, in0, scalar1, op0, R, W):
        self.S.add(eng, lambda e: e.tensor_scalar(out=out, in0=in0, scalar1=scalar1, scalar2=None, op0=op0), R, W)

    def memset(self, eng, ap, val, W):
        self.S.add(eng, lambda e: e.memset(ap, val), [], W)

    def recip(self, out, in_, R, W):
        self.S.add("dve", lambda e: e.reciprocal(out=out, in_=in_), R, W)

    def rsum(self, out, in_, R, W):
        self.S.add("dve", lambda e: e.reduce_sum(out=out, in_=in_, axis=AX.X), R, W)

    def build(self):
        try:
            self._build()
        except StopIteration:
            pass
        self.S.emit(self.nc)
        return self.nc

    def chk(self, tag):
        if self.stop == tag:
            raise StopIteration()

    def _build(self):
        nc = self.nc
        nl = self.nl
        x_d = self.din("x", [T, D])
        cvec_d = self.din("cvec", [D])
        wada_d = self.din("w_ada", [nl, D, 6 * D])
        bada_d = self.din("b_ada", [nl, 6 * D])
        nattn_d = self.din("norm_attn", [nl, D])
        nffn_d = self.din("norm_ffn", [nl, D])
        win_d = self.din("w_in", [nl, D, WIN])
        qn_d = self.din("qnorm", [nl, 128])
        kn_d = self.din("knorm", [nl, 128])
        kvn_d = self.din("kvnorm", [nl, 256])
        wuk_d = self.din("w_uk", [nl, 256, 1024])
        wuv_d = self.din("w_uv", [nl, 256, 1024])
        wo_d = self.din("w_o", [nl, D, D])
        wg_d = self.din("w_gate", [nl, D, DFF])
        wu_d = self.din("w_up", [nl, D, DFF])
        wd_d = self.din("w_down", [nl, DFF, D])
        nfin_d = self.din("norm_final", [D])
        ck_d = self.din("cache_k", [nl, 256, 256])
        cv_d = self.din("cache_v", [nl, 256, 256])
        cc_d = self.din("cache_c", [nl, 256, 256])
        cr_d = self.din("cache_r", [nl, 256, 64])
        cosA_d = self.din("cosA", [128, T])
        sinA_d = self.din("sinA", [128, T])
        cosB_d = self.din("cosB", [128, T])
        sinB_d = self.din("sinB", [128, T])
        ident_d = self.din("ident", [128, 128])
        ra_d = self.din("rotA", [128, 128])
        rb_d = self.din("rotB", [128, 128])
        mA_d = self.din("maskA", [9, NKEY])
        mB_d = self.din("maskB", [9, T])

        y_d = self.dout("y", [T, D])
        onk_d = self.dout("onk", [L, T, 256])
        onv_d = self.dout("onv", [L, T, 256])
        onc_d = self.dout("onc", [L, T, 256])
        onr_d = self.dout("onr", [L, T, 64])
        xres_d = self.dout("xres", [T, D])
        gscr_d = nc.dram_tensor("gscr", [L, 2, D], F32).ap()
        knscr_d = nc.dram_tensor("knscr", [8, 128, NKEY], BF16).ap()
        vbscr_d = nc.dram_tensor("vbscr", [8, 128, NKB * 128], BF16).ap()
        B_xres = Buf("xres")
        B_gscr = Buf("gscr")
        B_knscr = Buf("knscr")
        B_vbscr = Buf("vbscr")
        B_out = Buf("outs")

        ps = [TB(nc.alloc_psum_tensor("ps%d" % i, [128, 1024], BF16), "ps%d" % i) for i in range(2)]
        ps += [TB(nc.alloc_psum_tensor("ps%d" % i, [128, 512], F32), "ps%d" % i) for i in range(2, 8)]

        for p_ in ps:
            p_.b.excl = True
        ident_bf = self.sb("ident_bf", [128, 128], BF16)
        ident_f = self.sb("ident_f", [128, 128], F32)
        ones_bf = self.sb("ones_bf", [128, 128], BF16)
        ra_bf = self.sb("ra_bf", [128, 128], BF16)
        rb_bf = self.sb("rb_bf", [128, 128], BF16)
        mA = self.sb("mA", [9, NKEY], BF16)
        mB = self.sb("mB", [9, T], BF16)
        epsb = self.sb("epsb", [128, 1], F32)
        T1 = self.sb("T1", [128, 128], F32)
        T2 = self.sb("T2", [128, 32], F32)
        scol = self.sb("scol", [128, 16], F32)
        srep = self.sb("srep", [128, 16, 128], BF16)
        mcols = self.sb("mcols", [128, L * 4 * 16], F32)
        acols = self.sb("acols", [128, L * 2 * 16], F32)
        gtab = self.sb("gtab", [128, D], F32)
        xtl = [self.sb("xt%d" % i, [128, D], F32) for i in range(2)]
        xt = xtl[0]
        xn = [self.sb("xn%d" % i, [128, D], BF16) for i in range(2)]
        ssb = [self.sb("ss%d" % i, [128, 1], F32) for i in range(2)]
        rsb = [self.sb("rs%d" % i, [128, 1], F32) for i in range(2)]
        hT = self.sb("hT", [128, NKC, 512], BF16)
        xs = [self.sb("xs%d" % i, [128, 512], F32) for i in range(2)]
        xtmp = [self.sb("xtmp%d" % i, [128, 512], F32) for i in range(2)]
        st1 = self.sb("st1", [128, 128], F32)
        st2 = self.sb("st2", [32, 128], F32)
        work0 = self.off

        def mc(l, j):
            return mcols.t[:, (l * 4 + j) * 16:(l * 4 + j + 1) * 16]

        def ac(l, j):
            return acols.t[:, (l * 2 + j) * 16:(l * 2 + j + 1) * 16]

        self.dma("pool", ident_bf.t[:], ident_d, writes=[ident_bf.b])
        self.dma("pool", ra_bf.t[:], ra_d, writes=[ra_bf.b])
        self.dma("pool", rb_bf.t[:], rb_d, writes=[rb_bf.b])
        self.dma("pool", mA.t[:], mA_d, writes=[mA.b])
        self.dma("pool", mB.t[:], mB_d, writes=[mB.b])
        self.dma("sp", ident_f.t[:], ident_d, writes=[ident_f.b])
        self.memset("dve", ones_bf.t[:], 1.0, [ones_bf.b])
        self.memset("dve", epsb.t[:], EPS, [epsb.b])
        for i in range(4):
            self.dma("sp", xres_d[i * 512:(i + 1) * 512, :], x_d[i * 512:(i + 1) * 512, :], writes=[B_xres])
        self.memset("dve", st1.t[:], 0.0, [st1.b])
        self.memset("dve", st2.t[:], 0.0, [st2.b])
        self.dma("sp", st1.t[0:16 * nl, :], nattn_d.rearrange("l (k p) -> (l k) p", p=128), writes=[st1.b])
        self.dma("sp", st1.t[64:64 + 16 * nl, :], nffn_d.rearrange("l (k p) -> (l k) p", p=128), writes=[st1.b])
        self.dma("sp", st2.t[0:16, :], cvec_d.rearrange("(k p) -> k p", p=128), writes=[st2.b])
        self.dma("sp", st2.t[16:16 + nl, :], qn_d, writes=[st2.b])
        self.dma("sp", st2.t[20:20 + nl, :], kn_d, writes=[st2.b])
        self.dma("sp", st2.t[24:24 + 2 * nl, :], kvn_d.rearrange("l (j p) -> (l j) p", p=128), writes=[st2.b])
        self.tr([(ps[4].t[:, 0:128], st1.t[:], ident_f.t[:])], [st1.b, ident_f.b], [ps[4].b])
        self.cp("dve", T1.t[:], ps[4].t[:, 0:128], [ps[4].b], [T1.b])
        self.tr([(ps[5].t[:, 0:32], st2.t[:], ident_f.t[0:32, 0:32])], [st2.b, ident_f.b], [ps[5].b])
        self.cp("dve", T2.t[:], ps[5].t[:, 0:32], [ps[5].b], [T2.b])
        self.act(scol.t[:], T2.t[:, 0:16], AF.Silu, [T2.b], [scol.b])
        self.cp("dve", srep.t[:], scol.t[:].unsqueeze(2).to_broadcast([128, 16, 128]), [scol.b], [srep.b])

        self.chk("setup")
        self.off = work0
        wada = [self.sb("wada%d" % i, [128, NKC, 512], BF16) for i in range(2)]
        bbc = [self.sb("bbc%d" % i, [128, 512], F32) for i in range(2)]
        modt = [self.sb("modt%d" % i, [128, 512], F32) for i in range(2)]
        dtmp = [self.sb("dtmp%d" % i, [128, 128], F32) for i in range(2)]
        it = 0
        if self.dbg:
            self.memset("dve", mcols.t[:], 0.0, [mcols.b])
            self.memset("dve", acols.t[:], 1.0, [acols.b])
        for l in range(0 if self.dbg else nl):
            for blk in range(24):
                i2 = it % 2
                it += 1
                c0 = blk * 512
                self.dma("pool", wada[i2].t[:], wada_d[l, :, c0:c0 + 512].rearrange("(k p) c -> p k c", p=128),
                         writes=[wada[i2].b])
                self.dma("sp", bbc[i2].t[:], bada_d[l:l + 1, c0:c0 + 512].to_broadcast([128, 512]), writes=[bbc[i2].b])
                pb = ps[2 + i2]
                self.mmg(pb.t[:], [(srep.t[:, k, :], wada[i2].t[:, k, :]) for k in range(NKC)],
                         [srep.b, wada[i2].b], [pb.b])
                self.tt("dve", modt[i2].t[:], pb.t[:], bbc[i2].t[:], ALU.add, [pb.b, bbc[i2].b], [modt[i2].b])
                kind, q = blk // 4, blk % 4
                if kind in (2, 5):
                    self.dma("sp", gscr_d[l, kind // 3:kind // 3 + 1, q * 512:(q + 1) * 512], modt[i2].t[0:1, :],
                             reads=[modt[i2].b], writes=[B_gscr])
                else:
                    j = {0: 0, 1: 1, 3: 2, 4: 3}[kind]
                    for s in range(4):
                        kidx = q * 4 + s
                        dt_ = dtmp[s % 2]
                        self.tt("dve", dt_.t[:], modt[i2].t[:, s * 128:(s + 1) * 128], ident_f.t[:], ALU.mult,
                                [modt[i2].b, ident_f.b], [dt_.b])
                        self.rsum(mc(l, j)[:, kidx:kidx + 1], dt_.t[:], [dt_.b], [mcols.b])
            for j in range(2):
                self.stt("dve", ac(l, j), mc(l, 1 + 2 * j), 1.0, T1.t[:, j * 64 + l * 16:j * 64 + (l + 1) * 16],
                         ALU.add, ALU.mult, [mcols.b, T1.b], [acols.b])
        self.S.barrier()
        self.chk("mod")

        self.off = work0
        kT = self.sb("kT", [128, 2, NKEY], BF16)
        vst = self.sb("vst", [128, NKB, 256], BF16)
        krT = self.sb("krT", [128, NKEY], BF16)
        wblk = [self.sb("wblk%d" % i, [128, NKC, 128], BF16) for i in range(3)]
        rope = [self.sb("rope%d" % i, [128, 512], F32) for i in range(4)]
        sq = self.sb("sq", [128, 512], BF16)
        sq2 = self.sb("sq2", [128, 512], BF16)
        rr = self.sb("rr", [128, 512], F32)
        knf = self.sb("knf", [128, 512], F32)
        knb = self.sb("knb", [128, 512], BF16)
        t1 = self.sb("t1", [128, 512], F32)
        t2 = self.sb("t2", [128, 512], F32)
        fin = [self.sb("fin%d" % i, [128, 512], F32) for i in range(5)]
        ostage = [self.sb("ostage%d" % i, [128, 512], F32) for i in range(2)]
        ostage_r = self.sb("ostage_r", [128, 128], F32)
        ostage_v = self.sb("ostage_v", [128, 256], F32)
        cst = self.sb("cst", [128, 2, 256], F32)
        cst_r = self.sb("cst_r", [128, 2, 128], F32)
        abmark = self.off
        ckvT = self.sb("ckvT", [128, 2, NKEY], BF16)
        wuk = self.sb("wuk", [128, 2, 1024], BF16)
        wuv = self.sb("wuv", [128, 2, 1024], BF16)
        knx = self.sb("knx", [128, NKEY], BF16)
        vbx = self.sb("vbx", [128, NKB * 128], BF16)
        a_end = self.off
        self.off = abmark
        knT = [self.sb("knT%d" % i, [128, NKEY], BF16) for i in range(2)]
        vbh = [self.sb("vbh%d" % i, [128, NKB, 128], BF16) for i in range(2)]
        qT = [self.sb("qT%d" % i, [128, 512], BF16) for i in range(2)]
        qnT = [self.sb("qnT%d" % i, [128, 512], BF16) for i in range(2)]
        qrT = [self.sb("qrT%d" % i, [128, 512], BF16) for i in range(2)]
        pT = [self.sb("pT%d" % i, [128, 512], BF16) for i in range(3)]
        rinv = self.sb("rinv", [128, 512], F32)
        oT = [self.sb("oT%d" % i, [128, 512], BF16) for i in range(16)]
        wo = [self.sb("wo%d" % i, [128, NKC, 256], BF16) for i in range(2)]
        b_end = self.off
        self.off = work0
        gT = [self.sb("gT%d" % i, [128, 512], BF16) for i in range(NFC)]
        wgb = [self.sb("wg%d" % i, [128, NKC, 256], BF16) for i in range(3)]
        wub = [self.sb("wu%d" % i, [128, NKC, 256], BF16) for i in range(3)]
        wdb = [self.sb("wd%d" % i, [128, 4, 512], BF16) for i in range(6)]
        sg = [self.sb("sg%d" % i, [128, 512], F32) for i in range(2)]
        c_end = self.off
        self.off = max(a_end, b_end, c_end)
        self.sbuf_used = self.off

        def rms_stats(src, ss_, rs_, n, junk):
            self.memset("dve", ss_.t[:], 0.0, [ss_.b])
            self.act(junk.t[:], src.t[:], AF.Square, [src.b, ss_.b], [ss_.b, junk.b], accum=ss_.t[:])
            self.act(rs_.t[:], ss_.t[:], AF.Ln, [ss_.b, epsb.b], [rs_.b], scale=1.0 / n, bias=epsb.t[:])
            self.act(rs_.t[:], rs_.t[:], AF.Exp, [rs_.b], [rs_.b], scale=-0.5)

        def norm_chunk(l, c, j):
            A = ac(l, j)
            sh = mc(l, 2 * j)
            for tt in range(4):
                t = 4 * c + tt
                xnb = xn[tt % 2]
                xt = xtl[tt % 2]
                ss_, rs_ = ssb[tt % 2], rsb[tt % 2]
                self.dma("sp", xt.t[:], xres_d[t * 128:(t + 1) * 128, :], reads=[B_xres], writes=[xt.b])
                rms_stats(xt, ss_, rs_, float(D), xnb)
                self.ts1("dve", xnb.t[:], xt.t[:], rs_.t[:, 0:1], ALU.mult, [xt.b, rs_.b], [xnb.b])
                for half in range(2):
                    pb = ps[half]
                    self.tr([(pb.t[:, jj * 128:(jj + 1) * 128], xnb.t[:, (half * 8 + jj) * 128:(half * 8 + jj + 1) * 128], ident_bf.t[:])
                             for jj in range(8)], [xnb.b, ident_bf.b], [pb.b])
                    for jj in range(8):
                        k = half * 8 + jj
                        self.act(hT.t[:, k, tt * 128:(tt + 1) * 128], pb.t[:, jj * 128:(jj + 1) * 128], AF.Identity,
                                 [pb.b, acols.b, mcols.b], [hT.b], scale=A[:, k:k + 1], bias=sh[:, k:k + 1])

        wblk_i = [0]

        def load_wblk(l, blk):
            w = wblk[wblk_i[0] % len(wblk)]
            wblk_i[0] += 1
            self.dma("pool", w.t[:], win_d[l, :, blk * 128:(blk + 1) * 128].rearrange("(k p) c -> p k c", p=128),
                     writes=[w.b])
            return w

        def proj_block(w, pb):
            self.mmg(pb.t[:], [(w.t[:, k, :], hT.t[:, k, :]) for k in range(NKC)], [w.b, hT.b], [pb.b])

        def rstd_mm(sq_list, pbk):
            self.mmg(pbk.t[:], [(ones_bf.t[:], s.t[:]) for s in sq_list], [ones_bf.b] + [s.b for s in sq_list], [pbk.b])

        def rstd_act(n, pbk):
            self.act(rr.t[:], pbk.t[:], AF.Ln, [pbk.b, epsb.b], [rr.b], scale=1.0 / n, bias=epsb.t[:])
            self.act(rr.t[:], rr.t[:], AF.Exp, [rr.b], [rr.b], scale=-0.5)

        def rstd_from_sq(sq_list, n, pbk=None):
            pbk = pbk or ps[4]
            rstd_mm(sq_list, pbk)
            rstd_act(n, pbk)

        def rope_mm(src_b, rot_bf, pbk):
            self.mmg(pbk.t[:], [(rot_bf.t[:], src_b.t[:])], [rot_bf.b, src_b.b], [pbk.b])

        def rope_fin(src_f, cos_t, sin_t, out_ap, out_bufs, pbk):
            self.tt("dve", t1.t[:], src_f.t[:], cos_t.t[:], ALU.mult, [src_f.b, cos_t.b], [t1.b])
            self.tt("dve", t2.t[:], pbk.t[:], sin_t.t[:], ALU.mult, [pbk.b, sin_t.b], [t2.b])
            self.tt("dve", out_ap, t1.t[:], t2.t[:], ALU.add, [t1.b, t2.b], out_bufs)

        def rope_apply(src_f, src_b, rot_bf, cos_t, sin_t, out_ap, out_bufs, pbk=None):
            pbk = pbk or ps[5]
            rope_mm(src_b, rot_bf, pbk)
            rope_fin(src_f, cos_t, sin_t, out_ap, out_bufs, pbk)

        def load_rope(c):
            for i, d_ in enumerate((cosA_d, sinA_d, cosB_d, sinB_d)):
                self.dma("sp", rope[i].t[:], d_[:, c * 512:(c + 1) * 512], writes=[rope[i].b])

        def resid_update(pb, t, nb, ncols, i2):
            c0 = nb * ncols
            xs_, xm_ = xs[i2], xtmp[i2]
            self.dma("sp", xs_.t[:, 0:ncols], xres_d[t * 128:(t + 1) * 128, c0:c0 + ncols], reads=[B_xres], writes=[xs_.b])
            self.tt("dve", xm_.t[:, 0:ncols], pb.t[:, 0:ncols], gtab.t[:, c0:c0 + ncols], ALU.mult, [pb.b, gtab.b], [xm_.b])
            self.tt("dve", xs_.t[:, 0:ncols], xs_.t[:, 0:ncols], xm_.t[:, 0:ncols], ALU.add, [xs_.b, xm_.b], [xs_.b])
            self.dma("sp", xres_d[t * 128:(t + 1) * 128, c0:c0 + ncols], xs_.t[:, 0:ncols], reads=[xs_.b], writes=[B_xres])

        def attention(c, hidx, score_parts, v_of_kb, scale, v_bufs, k_bufs, q_bufs, inject):
            pO, pL = ps[6], ps[7]

            def emitS(kb):
                pS = ps[4 + kb % 2]
                pairs = [(lk(kb), rq) for (lk, rq) in score_parts]
                pairs.append((mA.t[0:9, kb * 128:(kb + 1) * 128], mB.t[0:9, c * 512:(c + 1) * 512]))
                self.mmg(pS.t[:], pairs, k_bufs + q_bufs + [mA.b, mB.b], [pS.b])

            emitS(0)
            emitS(1)
            if 0 in inject:
                inject[0]()
            def emitPV(kb):
                p_ = pT[kb % 3]
                self.mm([(pO.t[:], v_of_kb(kb), p_.t[:], kb == 0, kb == NKB - 1)], v_bufs + [p_.b], [pO.b])
                self.mm([(pL.t[:], ones_bf.t[:], p_.t[:], kb == 0, kb == NKB - 1)], [ones_bf.b, p_.b], [pL.b])

            for kb in range(NKB):
                if kb >= 1 and kb + 1 < NKB:
                    emitS(kb + 1)
                pS = ps[4 + kb % 2]
                p_ = pT[kb % 3]
                self.act(p_.t[:], pS.t[:], AF.Exp, [pS.b], [p_.b], scale=scale)
                if kb >= 1:
                    emitPV(kb - 1)
                if kb > 0 and kb in inject:
                    inject[kb]()
            emitPV(NKB - 1)
            self.recip(rinv.t[:], pL.t[:], [pL.b], [rinv.b])
            o_ = oT[hidx]
            self.tt("dve", o_.t[:], pO.t[:], rinv.t[:], ALU.mult, [pO.b, rinv.b], [o_.b])

        for l in range(nl):
            qg = T2.t[:, 16 + l:17 + l]
            kg = T2.t[:, 20 + l:21 + l]
            self.dma("pool", wuk.t[:], wuk_d[l].rearrange("(j p) n -> p j n", p=128), writes=[wuk.b])
            self.dma("pool", wuv.t[:], wuv_d[l].rearrange("(j p) n -> p j n", p=128), writes=[wuv.b])
            self.dma("pool", vst.t[:, 0:2, :], cv_d[l].rearrange("(kt p) n -> p kt n", p=128), writes=[vst.b])
            self.dma("sp", cst.t[:], ck_d[l].rearrange("(kt p) n -> p kt n", p=128), writes=[cst.b])
            self.tr([(ps[4].t[:, (h * 2 + kt) * 128:(h * 2 + kt + 1) * 128], cst.t[:, kt, h * 128:(h + 1) * 128], ident_f.t[:])
                     for h in range(2) for kt in range(2)], [cst.b, ident_f.b], [ps[4].b])
            self.cp("act", kT.t[:, :, 0:256], ps[4].t[:].rearrange("p (h n) -> p h n", h=2), [ps[4].b], [kT.b])
            self.dma("sp", cst.t[:], cc_d[l].rearrange("(kt p) n -> p kt n", p=128), writes=[cst.b])
            self.tr([(ps[4].t[:, (h * 2 + kt) * 128:(h * 2 + kt + 1) * 128], cst.t[:, kt, h * 128:(h + 1) * 128], ident_f.t[:])
                     for h in range(2) for kt in range(2)], [cst.b, ident_f.b], [ps[4].b])
            self.cp("act", ckvT.t[:, :, 0:256], ps[4].t[:].rearrange("p (h n) -> p h n", h=2), [ps[4].b], [ckvT.b])
            self.dma("sp", cst_r.t[:, :, 0:64], cr_d[l].rearrange("(kt p) n -> p kt n", p=128), writes=[cst_r.b])
            self.dma("sp", cst_r.t[:, :, 64:128], cr_d[l].rearrange("(kt p) n -> p kt n", p=128), writes=[cst_r.b])
            self.tr([(ps[5].t[:, kt * 128:(kt + 1) * 128], cst_r.t[:, kt, :], ident_f.t[:]) for kt in range(2)],
                    [cst_r.b, ident_f.b], [ps[5].b])
            self.cp("act", krT.t[:, 0:256], ps[5].t[:, 0:256], [ps[5].b], [krT.b])

            self.chk("loads")
            for c in range(4):
                norm_chunk(l, c, 0)
                self.chk("norm")
                load_rope(c)
                kc0 = 256 + c * 512
                for h in range(2):
                    w = load_wblk(l, 20 + h)
                    pb = ps[2 + h]
                    proj_block(w, pb)
                    self.act(sq.t[:], pb.t[:], AF.Square, [pb.b], [sq.b])
                    rstd_from_sq([sq], 128.0)
                    self.stt("dve", knf.t[:], pb.t[:], kg, rr.t[:], ALU.mult, ALU.mult, [pb.b, T2.b, rr.b], [knf.b])
                    self.cp("act", knb.t[:], knf.t[:], [knf.b], [knb.b])
                    rope_apply(knf, knb, ra_bf, rope[0], rope[1], fin[h].t[:], [fin[h].b])
                    self.cp("act", kT.t[:, h, kc0:kc0 + 512], fin[h].t[:], [fin[h].b], [kT.b])
                self.chk("ka")
                wc0 = load_wblk(l, 22)
                wc1 = load_wblk(l, 23)
                proj_block(wc0, ps[2])
                proj_block(wc1, ps[3])
                self.act(sq.t[:], ps[2].t[:], AF.Square, [ps[2].b], [sq.b])
                self.act(sq2.t[:], ps[3].t[:], AF.Square, [ps[3].b], [sq2.b])
                rstd_from_sq([sq, sq2], 256.0)
                for j in range(2):
                    cg = T2.t[:, 24 + l * 2 + j:25 + l * 2 + j]
                    self.stt("dve", fin[2 + j].t[:], ps[2 + j].t[:], cg, rr.t[:], ALU.mult, ALU.mult,
                             [ps[2 + j].b, T2.b, rr.b], [fin[2 + j].b])
                    self.cp("act", ckvT.t[:, j, kc0:kc0 + 512], fin[2 + j].t[:], [fin[2 + j].b], [ckvT.b])
                wr = load_wblk(l, 24)
                proj_block(wr, ps[2])
                self.cp("act", knf.t[:], ps[2].t[:], [ps[2].b], [knf.b])
                self.cp("dve", knb.t[:], ps[2].t[:], [ps[2].b], [knb.b])
                rope_apply(knf, knb, rb_bf, rope[2], rope[3], fin[4].t[:], [fin[4].b])
                self.cp("act", krT.t[:, kc0:kc0 + 512], fin[4].t[:], [fin[4].b], [krT.b])
                self.chk("kr")
                wv0 = load_wblk(l, 25)
                wv1 = load_wblk(l, 26)
                for tt in range(4):
                    t = 4 * c + tt
                    pv = ps[6]
                    items = []
                    for half, wv in enumerate((wv0, wv1)):
                        for k in range(NKC):
                            items.append((pv.t[:, half * 128:(half + 1) * 128], hT.t[:, k, tt * 128:(tt + 1) * 128], wv.t[:, k, :], k == 0, k == NKC - 1))
                    self.mm(items, [hT.b, wv0.b, wv1.b], [pv.b])
                    self.cp("act", vst.t[:, 2 + t, :], pv.t[:, 0:256], [pv.b], [vst.b])
                    self.cp("dve", ostage_v.t[:], pv.t[:, 0:256], [pv.b], [ostage_v.b])
                    self.dma("sp", onv_d[l, t * 128:(t + 1) * 128, :], ostage_v.t[:], reads=[ostage_v.b], writes=[B_out])
                    po = ps[7]
                    self.tr([(po.t[:, i * 128:(i + 1) * 128], fin[i].t[:, tt * 128:(tt + 1) * 128], ident_f.t[:]) for i in range(4)],
                            [fin[0].b, fin[1].b, fin[2].b, fin[3].b, ident_f.b], [po.b])
                    og = ostage[tt % 2]
                    self.cp("act", og.t[:], po.t[:], [po.b], [og.b])
                    self.dma("sp", onk_d[l, t * 128:(t + 1) * 128, :], og.t[:, 0:256], reads=[og.b], writes=[B_out])
                    self.dma("sp", onc_d[l, t * 128:(t + 1) * 128, :], og.t[:, 256:512], reads=[og.b], writes=[B_out])
                    self.tr([(po.t[:, 0:128], fin[4].t[:, tt * 128:(tt + 1) * 128], ident_f.t[:])], [fin[4].b, ident_f.b], [po.b])
                    self.cp("dve", ostage_r.t[:], po.t[:, 0:128], [po.b], [ostage_r.b])
                    self.dma("sp", onr_d[l, t * 128:(t + 1) * 128, :], ostage_r.t[:, 0:64], reads=[ostage_r.b], writes=[B_out])

            self.chk("passA")
            for h in range(8):
                for g in range(5):
                    n0 = g * 512
                    nn = 512 if g < 4 else 256
                    pb = ps[2 + g % 2]
                    self.mmg(pb.t[:, 0:nn], [(wuk.t[:, j, h * 128:(h + 1) * 128], ckvT.t[:, j, n0:n0 + nn]) for j in range(2)],
                             [wuk.b, ckvT.b], [pb.b])
                    self.cp("act" if g % 2 == 0 else "dve", knx.t[:, n0:n0 + nn], pb.t[:, 0:nn], [pb.b], [knx.b])
                self.dma("sp", knscr_d[h], knx.t[:], reads=[knx.b], writes=[B_knscr])
                for g in range(5):
                    kb0 = g * 4
                    nk = 4 if g < 4 else 2
                    pb = ps[2 + g % 2]
                    items = []
                    for i in range(nk):
                        kb = kb0 + i
                        for j in range(2):
                            items.append((pb.t[:, i * 128:(i + 1) * 128], ckvT.t[:, j, kb * 128:(kb + 1) * 128], wuv.t[:, j, h * 128:(h + 1) * 128], j == 0, j == 1))
                    self.mm(items, [wuv.b, ckvT.b], [pb.b])
                    self.cp("act" if g % 2 == 0 else "dve", vbx.t[:, kb0 * 128:(kb0 + nk) * 128], pb.t[:, 0:nk * 128], [pb.b], [vbx.b])
                self.dma("sp", vbscr_d[h], vbx.t[:], reads=[vbx.b], writes=[B_vbscr])
            self.S.barrier()

            self.chk("expand")
            self.dma("sp", gtab.t[:], gscr_d[l, 0:1, :].to_broadcast([128, D]), reads=[B_gscr], writes=[gtab.b])
            for c in range(4):
                norm_chunk(l, c, 0)
                load_rope(c)
                def gqa_stages(h):
                    q_ = qT[h % 2]
                    st = {}

                    def s0():
                        w = load_wblk(l, h)
                        proj_block(w, ps[2])

                    def s1():
                        self.act(sq.t[:], ps[2].t[:], AF.Square, [ps[2].b], [sq.b])

                    def s2():
                        rstd_mm([sq], ps[3])

                    def s3():
                        rstd_act(128.0, ps[3])

                    def s4():
                        self.stt("dve", knf.t[:], ps[2].t[:], qg, rr.t[:], ALU.mult, ALU.mult, [ps[2].b, T2.b, rr.b], [knf.b])
                        self.cp("act", knb.t[:], knf.t[:], [knf.b], [knb.b])

                    def s5():
                        rope_mm(knb, ra_bf, ps[3])

                    def s6():
                        rope_fin(knf, rope[0], rope[1], q_.t[:], [q_.b], ps[3])

                    return [s0, s1, s2, s3, s4, s5, s6]

                def gqa_attn(h, inject):
                    q_ = qT[h % 2]
                    kv = h // 4
                    attention(c, h,
                              [((lambda kb, kv_=kv: kT.t[:, kv_, kb * 128:(kb + 1) * 128]), q_.t[:])],
                              (lambda kb, kv_=kv: vst.t[:, kb, kv_ * 128:(kv_ + 1) * 128]),
                              128.0 ** -0.5, [vst.b], [kT.b], [q_.b], inject)

                def mla_stages(h):
                    kn_, vb_ = knT[h % 2], vbh[h % 2]
                    qn_ = qnT[h % 2]
                    qr_ = qrT[(h // 2) % 2]
                    out = []

                    def d0():
                        self.dma("sp", kn_.t[:], knscr_d[h], reads=[B_knscr], writes=[kn_.b])
                        self.dma("sp", vb_.t[:], vbscr_d[h].rearrange("p (kb n) -> p kb n", n=128), reads=[B_vbscr], writes=[vb_.b])

                    if h % 2 == 0:
                        def r0():
                            w = load_wblk(l, 16 + h // 2)
                            proj_block(w, ps[3])

                        def r1():
                            self.cp("act", knf.t[:], ps[3].t[:], [ps[3].b], [knf.b])
                            self.cp("act", knb.t[:], knf.t[:], [knf.b], [knb.b])

                        def r2():
                            rope_mm(knb, rb_bf, ps[3])

                        def r3():
                            rope_fin(knf, rope[2], rope[3], qr_.t[:], [qr_.b], ps[3])

                        out += [r0, d0, r1, r2, r3]

                    def n0():
                        w = load_wblk(l, 8 + h)
                        proj_block(w, ps[2])

                    def n1():
                        self.cp("act", qn_.t[:], ps[2].t[:], [ps[2].b], [qn_.b])

                    if h % 2 == 0:
                        out += [n0, n1]
                    else:
                        out += [n0, d0, n1]
                    return out

                def mla_attn(h, inject):
                    kn_, vb_ = knT[h % 2], vbh[h % 2]
                    qn_ = qnT[h % 2]
                    qr_ = qrT[(h // 2) % 2]
                    r0_ = (h % 2) * 64
                    attention(c, 8 + h,
                              [((lambda kb: kn_.t[:, kb * 128:(kb + 1) * 128]), qn_.t[:]),
                               ((lambda kb: krT.t[r0_:r0_ + 64, kb * 128:(kb + 1) * 128]), qr_.t[r0_:r0_ + 64, :])],
                              (lambda kb: vb_.t[:, kb, :]),
                              192.0 ** -0.5, [vb_.b], [kn_.b, krT.b], [qn_.b, qr_.b], inject)

                heads = [("g", h) for h in range(8)] + [("m", h) for h in range(8)]

                def stages_of(i):
                    kind, h = heads[i]
                    return gqa_stages(h) if kind == "g" else mla_stages(h)

                for s_ in stages_of(0):
                    s_()
                for i in range(16):
                    inject = {}
                    if i + 1 < 16:
                        sl = stages_of(i + 1)
                        for j, s_ in enumerate(sl):
                            inject[2 * j] = s_
                    kind, h = heads[i]
                    if kind == "g":
                        gqa_attn(h, inject)
                    else:
                        mla_attn(h, inject)
                it2 = 0
                for nb in range(8):
                    wo_ = wo[nb % 2]
                    self.dma("pool", wo_.t[:], wo_d[l, :, nb * 256:(nb + 1) * 256].rearrange("(k p) c -> p k c", p=128), writes=[wo_.b])
                    for tt in range(4):
                        pb = ps[2 + it2 % 4]
                        self.mmg(pb.t[:, 0:256], [(oT[k].t[:, tt * 128:(tt + 1) * 128], wo_.t[:, k, :]) for k in range(NKC)],
                                 [o.b for o in oT] + [wo_.b], [pb.b])
                        resid_update(pb, 4 * c + tt, nb, 256, it2 % 2)
                        it2 += 1
            self.S.barrier()

            self.chk("passB")
            self.dma("sp", gtab.t[:], gscr_d[l, 1:2, :].to_broadcast([128, D]), reads=[B_gscr], writes=[gtab.b])
            for c in range(4):
                norm_chunk(l, c, 1)
                for fg in range(22):
                    wg_ = wgb[fg % 3]
                    wu_ = wub[fg % 3]
                    self.dma("pool", wg_.t[:], wg_d[l, :, fg * 256:(fg + 1) * 256].rearrange("(k p) c -> p k c", p=128), writes=[wg_.b])
                    self.dma("pool", wu_.t[:], wu_d[l, :, fg * 256:(fg + 1) * 256].rearrange("(k p) c -> p k c", p=128), writes=[wu_.b])
                    for sub in range(2):
                        fb = fg * 2 + sub
                        pa = ps[2 + fb % 2]
                        pu = ps[4 + fb % 2]
                        self.mmg(pa.t[:], [(wg_.t[:, k, sub * 128:(sub + 1) * 128], hT.t[:, k, :]) for k in range(NKC)], [wg_.b, hT.b], [pa.b])
                        self.mmg(pu.t[:], [(wu_.t[:, k, sub * 128:(sub + 1) * 128], hT.t[:, k, :]) for k in range(NKC)], [wu_.b, hT.b], [pu.b])
                        sg_ = sg[fb % 2]
                        self.act(sg_.t[:], pa.t[:], AF.Silu, [pa.b], [sg_.b])
                        self.tt("dve", gT[fb].t[:], sg_.t[:], pu.t[:], ALU.mult, [sg_.b, pu.b], [gT[fb].b])
                wdi = 0
                for nb in range(4):
                    for ks in range(11):
                        wd_ = wdb[wdi % 6]
                        wdi += 1
                        self.dma("pool", wd_.t[:], wd_d[l, ks * 512:(ks + 1) * 512, nb * 512:(nb + 1) * 512].rearrange("(j p) c -> p j c", p=128), writes=[wd_.b])
                        items = []
                        for tt in range(4):
                            for j in range(4):
                                f = ks * 4 + j
                                items.append((ps[4 + tt].t[:], gT[f].t[:, tt * 128:(tt + 1) * 128], wd_.t[:, j, :], f == 0, f == NFC - 1))
                        self.mm(items, [gT[ks * 4 + j].b for j in range(4)] + [wd_.b], [ps[4 + tt].b for tt in range(4)])
                    for tt in range(4):
                        resid_update(ps[4 + tt], 4 * c + tt, nb, 512, tt % 2)
            self.S.barrier()

        self.dma("sp", gtab.t[:], nfin_d.rearrange("(o d) -> o d", o=1).to_broadcast([128, D]), writes=[gtab.b])
        for t in range(16):
            ss_, rs_ = ssb[t % 2], rsb[t % 2]
            self.dma("sp", xt.t[:], xres_d[t * 128:(t + 1) * 128, :], reads=[B_xres], writes=[xt.b])
            rms_stats(xt, ss_, rs_, float(D), xn[t % 2])
            self.stt("dve", xt.t[:], xt.t[:], rs_.t[:, 0:1], gtab.t[:], ALU.mult, ALU.mult, [xt.b, rs_.b, gtab.b], [xt.b])
            self.dma("sp", y_d[t * 128:(t + 1) * 128, :], xt.t[:], reads=[xt.b], writes=[B_out])


def _rope_tables(prompt):
    def tables(dim):
        nf = dim // 4
        half = dim // 2
        if prompt:
            return np.ones((dim, T), np.float32), np.zeros((dim, T), np.float32)
        tpos = np.arange(T)
        row = (tpos // 64).astype(np.float32)
        col = (tpos % 64).astype(np.float32)
        inv = (np.float32(10000.0) ** (-np.arange(nf, dtype=np.float32) / np.float32(nf))).astype(np.float32)
        ang = np.concatenate([row[:, None] * inv[None, :], col[:, None] * inv[None, :]], axis=-1)
        cs, sn = np.cos(ang).astype(np.float32), np.sin(ang).astype(np.float32)
        idx = np.arange(dim) % half
        return np.ascontiguousarray(cs[:, idx].T), np.ascontiguousarray(sn[:, idx].T)

    ca, sa = tables(128)
    cb, sb_ = tables(64)
    cb = np.concatenate([cb, cb], axis=0)
    sb_ = np.concatenate([sb_, sb_], axis=0)
    return ca, sa, cb, sb_


def _rot_mats():
    ra = np.zeros((128, 128), np.float32)
    for d in range(64):
        ra[d + 64, d] = -1.0
        ra[d, d + 64] = 1.0
    rb = np.zeros((128, 128), np.float32)
    for blk in range(2):
        o = blk * 64
        for e in range(32):
            rb[o + e + 32, o + e] = -1.0
            rb[o + e, o + e + 32] = 1.0
    return ra, rb


def _masks(prompt):
    mA = np.zeros((9, NKEY), np.float32)
    mB = np.zeros((9, T), np.float32)
    if prompt:
        mA[8, 0:256] = 1.0
        for s in range(8):
            mA[s, 256 + s * 256:256 + (s + 1) * 256] = 1.0
        mB[:, :] = NEG
        for s in range(8):
            mB[s, s * 256:(s + 1) * 256] = 0.0
    return mA, mB


def _permute_w_in(w_in):
    qa = w_in[:, :, 0:1024]
    ka = w_in[:, :, 1024:1280]
    va = w_in[:, :, 1280:1536]
    nl_ = w_in.shape[0]
    qb = w_in[:, :, 1536:3072].reshape(nl_, D, 8, 192)
    qbn = qb[:, :, :, 0:128].reshape(nl_, D, 1024)
    qbr = qb[:, :, :, 128:192].reshape(nl_, D, 512)
    ckv = w_in[:, :, 3072:3328]
    kr = w_in[:, :, 3328:3392]
    return np.ascontiguousarray(np.concatenate([qa, qbn, qbr, ka, ckv, kr, kr, va], axis=2))


_NC_CACHE = {}


def _get_nc(n_layers=L):
    if n_layers not in _NC_CACHE:
        _NC_CACHE[n_layers] = MK(n_layers).build()
    return _NC_CACHE[n_layers]


def make_in_maps(inp, nl=L):
    f = lambda a: np.ascontiguousarray(np.asarray(a, dtype=np.float32))
    fl = lambda a: np.ascontiguousarray(np.asarray(a, dtype=np.float32)[:nl])
    shared = {
        "w_ada": fl(inp["w_ada"]), "b_ada": fl(inp["b_ada"]), "norm_attn": fl(inp["norm_attn"]),
        "norm_ffn": fl(inp["norm_ffn"]), "w_in": _permute_w_in(fl(inp["w_in"])), "qnorm": fl(inp["qnorm_a"]),
        "knorm": fl(inp["knorm_a"]), "kvnorm": fl(inp["kvnorm_b"]), "w_uk": fl(inp["w_uk_b"]), "w_uv": fl(inp["w_uv_b"]),
        "w_o": fl(inp["w_o"]), "w_gate": fl(inp["w_gate"]), "w_up": fl(inp["w_up"]), "w_down": fl(inp["w_down"]),
        "norm_final": f(inp["norm_final"]), "ident": np.eye(128, dtype=np.float32),
    }
    ra, rb = _rot_mats()
    shared["rotA"], shared["rotB"] = ra, rb
    xp = f(inp["x_prompt"])
    xsm = f(inp["x_sample"])
    ck, cv = f(inp["cache_k_a"]), f(inp["cache_v_a"])
    cc, cr = f(inp["cache_ckv_b"]), f(inp["cache_krope_b"])
    cvec_s, cvec_p = f(inp["c"]), f(inp["c_ctx"])
    tabs = {False: _rope_tables(False), True: _rope_tables(True)}
    msk = {False: _masks(False), True: _masks(True)}
    in_maps = []
    for core in range(8):
        prompt = core >= 4
        m = dict(shared)
        if prompt:
            i = core - 4
            m["x"] = np.ascontiguousarray(xp[i * 8:(i + 1) * 8].reshape(T, D))
            m["cvec"] = cvec_p
            m["cache_k"] = np.zeros((nl, 256, 256), np.float32)
            m["cache_v"] = np.zeros((nl, 256, 256), np.float32)
            m["cache_c"] = np.zeros((nl, 256, 256), np.float32)
            m["cache_r"] = np.zeros((nl, 256, 64), np.float32)
        else:
            b = core
            m["x"] = np.ascontiguousarray(xsm[b])
            m["cvec"] = np.ascontiguousarray(cvec_s[b])
            m["cache_k"] = np.ascontiguousarray(ck[b].reshape(L, 256, 256)[:nl])
            m["cache_v"] = np.ascontiguousarray(cv[b].reshape(L, 256, 256)[:nl])
            m["cache_c"] = np.ascontiguousarray(cc[b][:nl])
            m["cache_r"] = np.ascontiguousarray(cr[b][:nl])
        ca, sa, cb, sb_ = tabs[prompt]
        m["cosA"], m["sinA"], m["cosB"], m["sinB"] = ca, sa, cb, sb_
        m["maskA"], m["maskB"] = msk[prompt]
        in_maps.append(m)
    return in_maps


def assemble(results):
    y_sample = np.stack([results[b]["y"] for b in range(4)], axis=0)
    y_prompt = np.concatenate([results[4 + i]["y"].reshape(8, 256, D) for i in range(4)], axis=0)

    def gather(name, tail):
        parts = []
        for i in range(4):
            a = results[4 + i][name]
            a = a.reshape(L, 8, 256, *tail).transpose(1, 0, 2, *range(3, 3 + len(tail)))
            parts.append(a)
        return np.ascontiguousarray(np.concatenate(parts, axis=0))

    new_k = gather("onk", (2, 128))
    new_v = gather("onv", (2, 128))
    new_c = gather("onc", (256,))
    new_r = gather("onr", (64,))
    return (y_prompt.astype(np.float32), y_sample.astype(np.float32), new_k, new_v, new_c, new_r)


def kernel(**inputs):
    nc = _get_nc(L)
    in_maps = make_in_maps(inputs)
    res = run_bass_kernel_spmd(nc, in_maps, core_ids=list(range(8)))
    return assemble(res.results)
```

```python
import numpy as np
import contextlib
import concourse.bass as bass
import concourse.mybir as mybir
from concourse.bass_utils import run_bass_kernel_spmd

F32 = mybir.dt.float32
BF16 = mybir.dt.bfloat16
AF = mybir.ActivationFunctionType
ALU = mybir.AluOpType
AX = mybir.AxisListType

D = 2048
T = 2048
L = 4
DFF = 5632
NKC = 16
NFC = 44
NKEY = 2304
NKB = 18
WIN = 3456
EPS = 1e-6
NEG = -30000.0

ENGS = ("pe", "act", "dve", "pool", "sp")
NDMASEM = 16


class Buf:
    __slots__ = ("name", "last_w", "readers", "dma_readers", "excl")

    def __init__(self, name):
        self.excl = False
        self.name = name
        self.last_w = None
        self.readers = {}
        self.dma_readers = []


class Op:
    __slots__ = ("eng", "fn", "deps", "is_dma", "sem", "val", "needs_sig")

    def __init__(self, eng, fn, is_dma):
        self.eng = eng
        self.fn = fn
        self.deps = set()
        self.is_dma = is_dma
        self.sem = None
        self.val = 0
        self.needs_sig = False


class Sched:
    def __init__(self):
        self.streams = {e: [] for e in ENGS}
        self.ndma = {e: 0 for e in ENGS}
        self.dma_hist = {e: [] for e in ENGS}
        self.bar_deps = {}

    def add(self, eng, fn, reads=(), writes=(), dma=False):
        op = Op(eng, fn, dma)
        deps = op.deps
        bd = self.bar_deps.get(eng)
        if bd:
            deps.update(bd)
            self.bar_deps[eng] = None
        for b in reads:
            if b.last_w is not None:
                deps.add(b.last_w)
            if b.excl:
                for e2, r in b.readers.items():
                    if e2 != eng:
                        deps.add(r)
        for b in writes:
            if b.last_w is not None:
                deps.add(b.last_w)
            for r in b.readers.values():
                deps.add(r)
            for r in b.dma_readers:
                deps.add(r)
        for b in reads:
            if dma:
                b.dma_readers.append(op)
            else:
                b.readers[eng] = op
        for b in writes:
            b.last_w = op
            b.readers = {}
            b.dma_readers = []
        deps.discard(op)
        if dma:
            j = self.ndma[eng]
            self.ndma[eng] = j + 1
            op.sem = ("dma", eng, j % NDMASEM)
            op.val = 16 * (j // NDMASEM + 1)
            hist = self.dma_hist[eng]
            if j >= NDMASEM:
                deps.add(hist[j - NDMASEM])
            hist.append(op)
        self.streams[eng].append(op)
        return op

    def barrier(self):
        deps = set()
        for e in ENGS:
            comp = [op for op in self.streams[e][-64:] if not op.is_dma]
            if comp:
                deps.add(comp[-1])
            else:
                comp = [op for op in self.streams[e] if not op.is_dma]
                if comp:
                    deps.add(comp[-1])
            for d in self.dma_hist[e][-NDMASEM:]:
                deps.add(d)
        for e in ENGS:
            self.bar_deps[e] = set(deps) | (self.bar_deps.get(e) or set())

    def plan(self):
        for e in ENGS:
            for op in self.streams[e]:
                for d in op.deps:
                    if not d.is_dma:
                        d.needs_sig = True
        for e in ENGS:
            cnt = 0
            for op in self.streams[e]:
                if op.is_dma:
                    continue
                if op.needs_sig:
                    cnt += 1
                    op.sem = ("prog", e)
                    op.val = cnt

    def emit(self, nc):
        self.plan()
        with contextlib.ExitStack() as es:
            sems = {}
            for e in ENGS:
                sems[("prog", e)] = es.enter_context(nc.semaphore("prog_" + e))
                for i in range(NDMASEM):
                    if self.ndma[e] > i:
                        sems[("dma", e, i)] = es.enter_context(nc.semaphore("dma_%s_%d" % (e, i)))
            block = es.enter_context(nc.Block())
            streams = self.streams
            dma_hist = self.dma_hist

            def run_stream(ename, eng):
                waited = {}
                for op in streams[ename]:
                    need = {}
                    for d in op.deps:
                        if need.get(d.sem, 0) < d.val:
                            need[d.sem] = d.val
                    for k, v in need.items():
                        if waited.get(k, 0) < v:
                            eng.wait_ge(sems[k], v)
                            waited[k] = v
                    ins = op.fn(eng)
                    if op.is_dma:
                        ins.then_inc(sems[op.sem], 16)
                    elif op.needs_sig:
                        ins.then_inc(sems[op.sem], 1)
                last = {}
                for op in dma_hist[ename]:
                    last[op.sem] = op.val
                for k, v in last.items():
                    if waited.get(k, 0) < v:
                        eng.wait_ge(sems[k], v)

            @block.tensor
            def _(eng):
                run_stream("pe", eng)

            @block.scalar
            def _(eng):
                run_stream("act", eng)

            @block.vector
            def _(eng):
                run_stream("dve", eng)

            @block.gpsimd
            def _(eng):
                run_stream("pool", eng)

            @block.sync
            def _(eng):
                run_stream("sp", eng)


class TB:
    __slots__ = ("t", "b")

    def __init__(self, t, name):
        self.t = t
        self.b = Buf(name)


_DT_SIZE = {F32: 4, BF16: 2}


class MK:
    def __init__(self, n_layers=L, stop=None, dbg=False):
        self.dbg = dbg
        self.stop = stop
        self.nl = n_layers
        self.nc = bass.Bass("TRN2", target_bir_lowering=False)
        self.S = Sched()
        self.off = 16512
        self.uid = 0

    def sb(self, name, shape, dt):
        n = 1
        for s in shape[1:]:
            n *= s
        nbytes = (n * _DT_SIZE[dt] + 63) // 64 * 64
        self.uid += 1
        t = self.nc.alloc_sbuf_tensor_at("%s_%d" % (name, self.uid), list(shape), dt, offset=self.off)
        self.off += nbytes
        assert self.off <= 229344, ("SBUF overflow", name, self.off)
        return TB(t, name)

    def din(self, name, shape, dt=F32):
        if self.dbg and name in ("w_ada", "w_gate", "w_up", "w_down", "w_o"):
            return self.nc.dram_tensor(name, [self.nl, 8, 8], dt, kind="ExternalInput").ap()
        return self.nc.dram_tensor(name, list(shape), dt, kind="ExternalInput").ap()

    def dout(self, name, shape, dt=F32):
        return self.nc.dram_tensor(name, list(shape), dt, kind="ExternalOutput").ap()

    def dma(self, q, out_ap, in_ap, reads=(), writes=()):
        self.S.add(q, lambda e: e.dma_start(out=out_ap, in_=in_ap), reads, writes, dma=True)

    def mm(self, items, reads, writes):
        def fn(e):
            ins = None
            for (o, l, r, st, sp) in items:
                ins = e.matmul(o, lhsT=l, rhs=r, start=st, stop=sp)
            return ins

        self.S.add("pe", fn, reads, writes)

    def mmg(self, out, pairs, reads, writes):
        n = len(pairs)
        self.mm([(out, l, r, i == 0, i == n - 1) for i, (l, r) in enumerate(pairs)], reads, writes)

    def tr(self, items, reads, writes):
        def fn(e):
            ins = None
            for (o, i, idn) in items:
                ins = e.transpose(o, i, idn)
            return ins

        self.S.add("pe", fn, reads, writes)

    def act(self, out, in_, func, R, W, scale=None, bias=None, accum=None):
        kw = {}
        if scale is not None:
            kw["scale"] = scale
        if bias is not None:
            kw["bias"] = bias
        if accum is not None:
            kw["accum_out"] = accum
        self.S.add("act", lambda e: e.activation(out=out, in_=in_, func=func, **kw), R, W)

    def cp(self, eng, out, in_, R, W):
        if eng == "act":
            self.S.add("act", lambda e: e.copy(out=out, in_=in_), R, W)
        else:
            self.S.add(eng, lambda e: e.tensor_copy(out=out, in_=in_), R, W)

    def tt(self, eng, out, in0, in1, op, R, W):
        self.S.add(eng, lambda e: e.tensor_tensor(out=out, in0=in0, in1=in1, op=op), R, W)

    def stt(self, eng, out, in0, scalar, in1, op0, op1, R, W):
        self.S.add(eng, lambda e: e.scalar_tensor_tensor(out=out, in0=in0, scalar=scalar, in1=in1, op0=op0, op1=op1), R, W)

    def ts1(self, eng, out, in0, scalar1, op0, R, W):
        self.S.add(eng, lambda e: e.tensor_scalar(out=out, in0=in0, scalar1=scalar1, scalar2=None, op0=op0), R, W)

    def memset(self, eng, ap, val, W):
        self.S.add(eng, lambda e: e.memset(ap, val), [], W)

    def recip(self, out, in_, R, W):
        self.S.add("dve", lambda e: e.reciprocal(out=out, in_=in_), R, W)

    def rsum(self, out, in_, R, W):
        self.S.add("dve", lambda e: e.reduce_sum(out=out, in_=in_, axis=AX.X), R, W)

    def build(self):
        try:
            self._build()
        except StopIteration:
            pass
        self.S.emit(self.nc)
        return self.nc

    def chk(self, tag):
        if self.stop == tag:
            raise StopIteration()

    def _build(self):
        nc = self.nc
        nl = self.nl
        x_d = self.din("x", [T, D])
        cvec_d = self.din("cvec", [D])
        wada_d = self.din("w_ada", [nl, D, 6 * D])
        bada_d = self.din("b_ada", [nl, 6 * D])
        nattn_d = self.din("norm_attn", [nl, D])
        nffn_d = self.din("norm_ffn", [nl, D])
        win_d = self.din("w_in", [nl, D, WIN])
        qn_d = self.din("qnorm", [nl, 128])
        kn_d = self.din("knorm", [nl, 128])
        kvn_d = self.din("kvnorm", [nl, 256])
        wuk_d = self.din("w_uk", [nl, 256, 1024])
        wuv_d = self.din("w_uv", [nl, 256, 1024])
        wo_d = self.din("w_o", [nl, D, D])
        wg_d = self.din("w_gate", [nl, D, DFF])
        wu_d = self.din("w_up", [nl, D, DFF])
        wd_d = self.din("w_down", [nl, DFF, D])
        nfin_d = self.din("norm_final", [D])
        ck_d = self.din("cache_k", [nl, 256, 256])
        cv_d = self.din("cache_v", [nl, 256, 256])
        cc_d = self.din("cache_c", [nl, 256, 256])
        cr_d = self.din("cache_r", [nl, 256, 64])
        cosA_d = self.din("cosA", [128, T])
        sinA_d = self.din("sinA", [128, T])
        cosB_d = self.din("cosB", [128, T])
        sinB_d = self.din("sinB", [128, T])
        ident_d = self.din("ident", [128, 128])
        ra_d = self.din("rotA", [128, 128])
        rb_d = self.din("rotB", [128, 128])
        mA_d = self.din("maskA", [9, NKEY])
        mB_d = self.din("maskB", [9, T])

        y_d = self.dout("y", [T, D])
        onk_d = self.dout("onk", [L, T, 256])
        onv_d = self.dout("onv", [L, T, 256])
        onc_d = self.dout("onc", [L, T, 256])
        onr_d = self.dout("onr", [L, T, 64])
        xres_d = self.dout("xres", [T, D])
        gscr_d = nc.dram_tensor("gscr", [L, 2, D], F32).ap()
        knscr_d = nc.dram_tensor("knscr", [8, 128, NKEY], BF16).ap()
        vbscr_d = nc.dram_tensor("vbscr", [8, 128, NKB * 128], BF16).ap()
        B_xres = Buf("xres")
        B_gscr = Buf("gscr")
        B_knscr = Buf("knscr")
        B_vbscr = Buf("vbscr")
        B_out = Buf("outs")

        ps = [TB(nc.alloc_psum_tensor("ps%d" % i, [128, 1024], BF16), "ps%d" % i) for i in range(2)]
        ps += [TB(nc.alloc_psum_tensor("ps%d" % i, [128, 512], F32), "ps%d" % i) for i in range(2, 8)]

        for p_ in ps:
            p_.b.excl = True
        ident_bf = self.sb("ident_bf", [128, 128], BF16)
        ident_f = self.sb("ident_f", [128, 128], F32)
        ones_bf = self.sb("ones_bf", [128, 128], BF16)
        ra_bf = self.sb("ra_bf", [128, 128], BF16)
        rb_bf = self.sb("rb_bf", [128, 128], BF16)
        mA = self.sb("mA", [9, NKEY], BF16)
        mB = self.sb("mB", [9, T], BF16)
        epsb = self.sb("epsb", [128, 1], F32)
        T1 = self.sb("T1", [128, 128], F32)
        T2 = self.sb("T2", [128, 32], F32)
        scol = self.sb("scol", [128, 16], F32)
        srep = self.sb("srep", [128, 16, 128], BF16)
        mcols = self.sb("mcols", [128, L * 4 * 16], F32)
        acols = self.sb("acols", [128, L * 2 * 16], F32)
        gtab = self.sb("gtab", [128, D], F32)
        xtl = [self.sb("xt%d" % i, [128, D], F32) for i in range(2)]
        xt = xtl[0]
        xn = [self.sb("xn%d" % i, [128, D], BF16) for i in range(2)]
        ssb = [self.sb("ss%d" % i, [128, 1], F32) for i in range(2)]
        rsb = [self.sb("rs%d" % i, [128, 1], F32) for i in range(2)]
        hT = self.sb("hT", [128, NKC, 512], BF16)
        xs = [self.sb("xs%d" % i, [128, 512], F32) for i in range(2)]
        xtmp = [self.sb("xtmp%d" % i, [128, 512], F32) for i in range(2)]
        st1 = self.sb("st1", [128, 128], F32)
        st2 = self.sb("st2", [32, 128], F32)
        work0 = self.off

        def mc(l, j):
            return mcols.t[:, (l * 4 + j) * 16:(l * 4 + j + 1) * 16]

        def ac(l, j):
            return acols.t[:, (l * 2 + j) * 16:(l * 2 + j + 1) * 16]

        self.dma("pool", ident_bf.t[:], ident_d, writes=[ident_bf.b])
        self.dma("pool", ra_bf.t[:], ra_d, writes=[ra_bf.b])
        self.dma("pool", rb_bf.t[:], rb_d, writes=[rb_bf.b])
        self.dma("pool", mA.t[:], mA_d, writes=[mA.b])
        self.dma("pool", mB.t[:], mB_d, writes=[mB.b])
        self.dma("sp", ident_f.t[:], ident_d, writes=[ident_f.b])
        self.memset("dve", ones_bf.t[:], 1.0, [ones_bf.b])
        self.memset("dve", epsb.t[:], EPS, [epsb.b])
        for i in range(4):
            self.dma("sp", xres_d[i * 512:(i + 1) * 512, :], x_d[i * 512:(i + 1) * 512, :], writes=[B_xres])
        self.memset("dve", st1.t[:], 0.0, [st1.b])
        self.memset("dve", st2.t[:], 0.0, [st2.b])
        self.dma("sp", st1.t[0:16 * nl, :], nattn_d.rearrange("l (k p) -> (l k) p", p=128), writes=[st1.b])
        self.dma("sp", st1.t[64:64 + 16 * nl, :], nffn_d.rearrange("l (k p) -> (l k) p", p=128), writes=[st1.b])
        self.dma("sp", st2.t[0:16, :], cvec_d.rearrange("(k p) -> k p", p=128), writes=[st2.b])
        self.dma("sp", st2.t[16:16 + nl, :], qn_d, writes=[st2.b])
        self.dma("sp", st2.t[20:20 + nl, :], kn_d, writes=[st2.b])
        self.dma("sp", st2.t[24:24 + 2 * nl, :], kvn_d.rearrange("l (j p) -> (l j) p", p=128), writes=[st2.b])
        self.tr([(ps[4].t[:, 0:128], st1.t[:], ident_f.t[:])], [st1.b, ident_f.b], [ps[4].b])
        self.cp("dve", T1.t[:], ps[4].t[:, 0:128], [ps[4].b], [T1.b])
        self.tr([(ps[5].t[:, 0:32], st2.t[:], ident_f.t[0:32, 0:32])], [st2.b, ident_f.b], [ps[5].b])
        self.cp("dve", T2.t[:], ps[5].t[:, 0:32], [ps[5].b], [T2.b])
        self.act(scol.t[:], T2.t[:, 0:16], AF.Silu, [T2.b], [scol.b])
        self.cp("dve", srep.t[:], scol.t[:].unsqueeze(2).to_broadcast([128, 16, 128]), [scol.b], [srep.b])

        self.chk("setup")
        self.off = work0
        wada = [self.sb("wada%d" % i, [128, NKC, 512], BF16) for i in range(2)]
        bbc = [self.sb("bbc%d" % i, [128, 512], F32) for i in range(2)]
        modt = [self.sb("modt%d" % i, [128, 512], F32) for i in range(2)]
        dtmp = [self.sb("dtmp%d" % i, [128, 128], F32) for i in range(2)]
        it = 0
        if self.dbg:
            self.memset("dve", mcols.t[:], 0.0, [mcols.b])
            self.memset("dve", acols.t[:], 1.0, [acols.b])
        for l in range(0 if self.dbg else nl):
            for blk in range(24):
                i2 = it % 2
                it += 1
                c0 = blk * 512
                self.dma("pool", wada[i2].t[:], wada_d[l, :, c0:c0 + 512].rearrange("(k p) c -> p k c", p=128),
                         writes=[wada[i2].b])
                self.dma("sp", bbc[i2].t[:], bada_d[l:l + 1, c0:c0 + 512].to_broadcast([128, 512]), writes=[bbc[i2].b])
                pb = ps[2 + i2]
                self.mmg(pb.t[:], [(srep.t[:, k, :], wada[i2].t[:, k, :]) for k in range(NKC)],
                         [srep.b, wada[i2].b], [pb.b])
                self.tt("dve", modt[i2].t[:], pb.t[:], bbc[i2].t[:], ALU.add, [pb.b, bbc[i2].b], [modt[i2].b])
                kind, q = blk // 4, blk % 4
                if kind in (2, 5):
                    self.dma("sp", gscr_d[l, kind // 3:kind // 3 + 1, q * 512:(q + 1) * 512], modt[i2].t[0:1, :],
                             reads=[modt[i2].b], writes=[B_gscr])
                else:
                    j = {0: 0, 1: 1, 3: 2, 4: 3}[kind]
                    for s in range(4):
                        kidx = q * 4 + s
                        dt_ = dtmp[s % 2]
                        self.tt("dve", dt_.t[:], modt[i2].t[:, s * 128:(s + 1) * 128], ident_f.t[:], ALU.mult,
                                [modt[i2].b, ident_f.b], [dt_.b])
                        self.rsum(mc(l, j)[:, kidx:kidx + 1], dt_.t[:], [dt_.b], [mcols.b])
            for j in range(2):
                self.stt("dve", ac(l, j), mc(l, 1 + 2 * j), 1.0, T1.t[:, j * 64 + l * 16:j * 64 + (l + 1) * 16],
                         ALU.add, ALU.mult, [mcols.b, T1.b], [acols.b])
        self.S.barrier()
        self.chk("mod")

        self.off = work0
        kT = self.sb("kT", [128, 2, NKEY], BF16)
        vst = self.sb("vst", [128, NKB, 256], BF16)
        krT = self.sb("krT", [128, NKEY], BF16)
        wblk = [self.sb("wblk%d" % i, [128, NKC, 128], BF16) for i in range(3)]
        rope = [self.sb("rope%d" % i, [128, 512], F32) for i in range(4)]
        sq = self.sb("sq", [128, 512], BF16)
        sq2 = self.sb("sq2", [128, 512], BF16)
        rr = self.sb("rr", [128, 512], F32)
        knf = self.sb("knf", [128, 512], F32)
        knb = self.sb("knb", [128, 512], BF16)
        t1 = self.sb("t1", [128, 512], F32)
        t2 = self.sb("t2", [128, 512], F32)
        fin = [self.sb("fin%d" % i, [128, 512], F32) for i in range(5)]
        ostage = [self.sb("ostage%d" % i, [128, 512], F32) for i in range(2)]
        ostage_r = self.sb("ostage_r", [128, 128], F32)
        ostage_v = self.sb("ostage_v", [128, 256], F32)
        cst = self.sb("cst", [128, 2, 256], F32)
        cst_r = self.sb("cst_r", [128, 2, 128], F32)
        abmark = self.off
        ckvT = self.sb("ckvT", [128, 2, NKEY], BF16)
        wuk = self.sb("wuk", [128, 2, 1024], BF16)
        wuv = self.sb("wuv", [128, 2, 1024], BF16)
        knx = self.sb("knx", [128, NKEY], BF16)
        vbx = self.sb("vbx", [128, NKB * 128], BF16)
        a_end = self.off
        self.off = abmark
        knT = [self.sb("knT%d" % i, [128, NKEY], BF16) for i in range(2)]
        vbh = [self.sb("vbh%d" % i, [128, NKB, 128], BF16) for i in range(2)]
        qT = [self.sb("qT%d" % i, [128, 512], BF16) for i in range(2)]
        qnT = [self.sb("qnT%d" % i, [128, 512], BF16) for i in range(2)]
        qrT = [self.sb("qrT%d" % i, [128, 512], BF16) for i in range(2)]
        pT = [self.sb("pT%d" % i, [128, 512], BF16) for i in range(3)]
        rinv = self.sb("rinv", [128, 512], F32)
        oT = [self.sb("oT%d" % i, [128, 512], BF16) for i in range(16)]
        wo = [self.sb("wo%d" % i, [128, NKC, 256], BF16) for i in range(2)]
        b_end = self.off
        self.off = work0
        gT = [self.sb("gT%d" % i, [128, 512], BF16) for i in range(NFC)]
        wgb = [self.sb("wg%d" % i, [128, NKC, 256], BF16) for i in range(3)]
        wub = [self.sb("wu%d" % i, [128, NKC, 256], BF16) for i in range(3)]
        wdb = [self.sb("wd%d" % i, [128, 4, 512], BF16) for i in range(6)]
        sg = [self.sb("sg%d" % i, [128, 512], F32) for i in range(2)]
        c_end = self.off
        self.off = max(a_end, b_end, c_end)
        self.sbuf_used = self.off

        def rms_stats(src, ss_, rs_, n, junk):
            self.memset("dve", ss_.t[:], 0.0, [ss_.b])
            self.act(junk.t[:], src.t[:], AF.Square, [src.b, ss_.b], [ss_.b, junk.b], accum=ss_.t[:])
            self.act(rs_.t[:], ss_.t[:], AF.Ln, [ss_.b, epsb.b], [rs_.b], scale=1.0 / n, bias=epsb.t[:])
            self.act(rs_.t[:], rs_.t[:], AF.Exp, [rs_.b], [rs_.b], scale=-0.5)

        def norm_chunk(l, c, j):
            A = ac(l, j)
            sh = mc(l, 2 * j)
            for tt in range(4):
                t = 4 * c + tt
                xnb = xn[tt % 2]
                xt = xtl[tt % 2]
                ss_, rs_ = ssb[tt % 2], rsb[tt % 2]
                self.dma("sp", xt.t[:], xres_d[t * 128:(t + 1) * 128, :], reads=[B_xres], writes=[xt.b])
                rms_stats(xt, ss_, rs_, float(D), xnb)
                self.ts1("dve", xnb.t[:], xt.t[:], rs_.t[:, 0:1], ALU.mult, [xt.b, rs_.b], [xnb.b])
                for half in range(2):
                    pb = ps[half]
                    self.tr([(pb.t[:, jj * 128:(jj + 1) * 128], xnb.t[:, (half * 8 + jj) * 128:(half * 8 + jj + 1) * 128], ident_bf.t[:])
                             for jj in range(8)], [xnb.b, ident_bf.b], [pb.b])
                    for jj in range(8):
                        k = half * 8 + jj
                        self.act(hT.t[:, k, tt * 128:(tt + 1) * 128], pb.t[:, jj * 128:(jj + 1) * 128], AF.Identity,
                                 [pb.b, acols.b, mcols.b], [hT.b], scale=A[:, k:k + 1], bias=sh[:, k:k + 1])

        wblk_i = [0]

        def load_wblk(l, blk):
            w = wblk[wblk_i[0] % len(wblk)]
            wblk_i[0] += 1
            self.dma("pool", w.t[:], win_d[l, :, blk * 128:(blk + 1) * 128].rearrange("(k p) c -> p k c", p=128),
                     writes=[w.b])
            return w

        def proj_block(w, pb):
            self.mmg(pb.t[:], [(w.t[:, k, :], hT.t[:, k, :]) for k in range(NKC)], [w.b, hT.b], [pb.b])

        def rstd_mm(sq_list, pbk):
            self.mmg(pbk.t[:], [(ones_bf.t[:], s.t[:]) for s in sq_list], [ones_bf.b] + [s.b for s in sq_list], [pbk.b])

        def rstd_act(n, pbk):
            self.act(rr.t[:], pbk.t[:], AF.Ln, [pbk.b, epsb.b], [rr.b], scale=1.0 / n, bias=epsb.t[:])
            self.act(rr.t[:], rr.t[:], AF.Exp, [rr.b], [rr.b], scale=-0.5)

        def rstd_from_sq(sq_list, n, pbk=None):
            pbk = pbk or ps[4]
            rstd_mm(sq_list, pbk)
            rstd_act(n, pbk)

        def rope_mm(src_b, rot_bf, pbk):
            self.mmg(pbk.t[:], [(rot_bf.t[:], src_b.t[:])], [rot_bf.b, src_b.b], [pbk.b])

        def rope_fin(src_f, cos_t, sin_t, out_ap, out_bufs, pbk):
            self.tt("dve", t1.t[:], src_f.t[:], cos_t.t[:], ALU.mult, [src_f.b, cos_t.b], [t1.b])
            self.tt("dve", t2.t[:], pbk.t[:], sin_t.t[:], ALU.mult, [pbk.b, sin_t.b], [t2.b])
            self.tt("dve", out_ap, t1.t[:], t2.t[:], ALU.add, [t1.b, t2.b], out_bufs)

        def rope_apply(src_f, src_b, rot_bf, cos_t, sin_t, out_ap, out_bufs, pbk=None):
            pbk = pbk or ps[5]
            rope_mm(src_b, rot_bf, pbk)
            rope_fin(src_f, cos_t, sin_t, out_ap, out_bufs, pbk)

        def load_rope(c):
            for i, d_ in enumerate((cosA_d, sinA_d, cosB_d, sinB_d)):
                self.dma("sp", rope[i].t[:], d_[:, c * 512:(c + 1) * 512], writes=[rope[i].b])

        def resid_update(pb, t, nb, ncols, i2):
            c0 = nb * ncols
            xs_, xm_ = xs[i2], xtmp[i2]
            self.dma("sp", xs_.t[:, 0:ncols], xres_d[t * 128:(t + 1) * 128, c0:c0 + ncols], reads=[B_xres], writes=[xs_.b])
            self.tt("dve", xm_.t[:, 0:ncols], pb.t[:, 0:ncols], gtab.t[:, c0:c0 + ncols], ALU.mult, [pb.b, gtab.b], [xm_.b])
            self.tt("dve", xs_.t[:, 0:ncols], xs_.t[:, 0:ncols], xm_.t[:, 0:ncols], ALU.add, [xs_.b, xm_.b], [xs_.b])
            self.dma("sp", xres_d[t * 128:(t + 1) * 128, c0:c0 + ncols], xs_.t[:, 0:ncols], reads=[xs_.b], writes=[B_xres])

        def attention(c, hidx, score_parts, v_of_kb, scale, v_bufs, k_bufs, q_bufs, inject):
            pO, pL = ps[6], ps[7]

            def emitS(kb):
                pS = ps[4 + kb % 2]
                pairs = [(lk(kb), rq) for (lk, rq) in score_parts]
                pairs.append((mA.t[0:9, kb * 128:(kb + 1) * 128], mB.t[0:9, c * 512:(c + 1) * 512]))
                self.mmg(pS.t[:], pairs, k_bufs + q_bufs + [mA.b, mB.b], [pS.b])

            emitS(0)
            emitS(1)
            if 0 in inject:
                inject[0]()
            def emitPV(kb):
                p_ = pT[kb % 3]
                self.mm([(pO.t[:], v_of_kb(kb), p_.t[:], kb == 0, kb == NKB - 1)], v_bufs + [p_.b], [pO.b])
                self.mm([(pL.t[:], ones_bf.t[:], p_.t[:], kb == 0, kb == NKB - 1)], [ones_bf.b, p_.b], [pL.b])

            for kb in range(NKB):
                if kb >= 1 and kb + 1 < NKB:
                    emitS(kb + 1)
                pS = ps[4 + kb % 2]
                p_ = pT[kb % 3]
                self.act(p_.t[:], pS.t[:], AF.Exp, [pS.b], [p_.b], scale=scale)
                if kb >= 1:
                    emitPV(kb - 1)
                if kb > 0 and kb in inject:
                    inject[kb]()
            emitPV(NKB - 1)
            self.recip(rinv.t[:], pL.t[:], [pL.b], [rinv.b])
            o_ = oT[hidx]
            self.tt("dve", o_.t[:], pO.t[:], rinv.t[:], ALU.mult, [pO.b, rinv.b], [o_.b])

        for l in range(nl):
            qg = T2.t[:, 16 + l:17 + l]
            kg = T2.t[:, 20 + l:21 + l]
            self.dma("pool", wuk.t[:], wuk_d[l].rearrange("(j p) n -> p j n", p=128), writes=[wuk.b])
            self.dma("pool", wuv.t[:], wuv_d[l].rearrange("(j p) n -> p j n", p=128), writes=[wuv.b])
            self.dma("pool", vst.t[:, 0:2, :], cv_d[l].rearrange("(kt p) n -> p kt n", p=128), writes=[vst.b])
            self.dma("sp", cst.t[:], ck_d[l].rearrange("(kt p) n -> p kt n", p=128), writes=[cst.b])
            self.tr([(ps[4].t[:, (h * 2 + kt) * 128:(h * 2 + kt + 1) * 128], cst.t[:, kt, h * 128:(h + 1) * 128], ident_f.t[:])
                     for h in range(2) for kt in range(2)], [cst.b, ident_f.b], [ps[4].b])
            self.cp("act", kT.t[:, :, 0:256], ps[4].t[:].rearrange("p (h n) -> p h n", h=2), [ps[4].b], [kT.b])
            self.dma("sp", cst.t[:], cc_d[l].rearrange("(kt p) n -> p kt n", p=128), writes=[cst.b])
            self.tr([(ps[4].t[:, (h * 2 + kt) * 128:(h * 2 + kt + 1) * 128], cst.t[:, kt, h * 128:(h + 1) * 128], ident_f.t[:])
                     for h in range(2) for kt in range(2)], [cst.b, ident_f.b], [ps[4].b])
            self.cp("act", ckvT.t[:, :, 0:256], ps[4].t[:].rearrange("p (h n) -> p h n", h=2), [ps[4].b], [ckvT.b])
            self.dma("sp", cst_r.t[:, :, 0:64], cr_d[l].rearrange("(kt p) n -> p kt n", p=128), writes=[cst_r.b])
            self.dma("sp", cst_r.t[:, :, 64:128], cr_d[l].rearrange("(kt p) n -> p kt n", p=128), writes=[cst_r.b])
            self.tr([(ps[5].t[:, kt * 128:(kt + 1) * 128], cst_r.t[:, kt, :], ident_f.t[:]) for kt in range(2)],
                    [cst_r.b, ident_f.b], [ps[5].b])
            self.cp("act", krT.t[:, 0:256], ps[5].t[:, 0:256], [ps[5].b], [krT.b])

            self.chk("loads")
            for c in range(4):
                norm_chunk(l, c, 0)
                self.chk("norm")
                load_rope(c)
                kc0 = 256 + c * 512
                for h in range(2):
                    w = load_wblk(l, 20 + h)
                    pb = ps[2 + h]
                    proj_block(w, pb)
                    self.act(sq.t[:], pb.t[:], AF.Square, [pb.b], [sq.b])
                    rstd_from_sq([sq], 128.0)
                    self.stt("dve", knf.t[:], pb.t[:], kg, rr.t[:], ALU.mult, ALU.mult, [pb.b, T2.b, rr.b], [knf.b])
                    self.cp("act", knb.t[:], knf.t[:], [knf.b], [knb.b])
                    rope_apply(knf, knb, ra_bf, rope[0], rope[1], fin[h].t[:], [fin[h].b])
                    self.cp("act", kT.t[:, h, kc0:kc0 + 512], fin[h].t[:], [fin[h].b], [kT.b])
                self.chk("ka")
                wc0 = load_wblk(l, 22)
                wc1 = load_wblk(l, 23)
                proj_block(wc0, ps[2])
                proj_block(wc1, ps[3])
                self.act(sq.t[:], ps[2].t[:], AF.Square, [ps[2].b], [sq.b])
                self.act(sq2.t[:], ps[3].t[:], AF.Square, [ps[3].b], [sq2.b])
                rstd_from_sq([sq, sq2], 256.0)
                for j in range(2):
                    cg = T2.t[:, 24 + l * 2 + j:25 + l * 2 + j]
                    self.stt("dve", fin[2 + j].t[:], ps[2 + j].t[:], cg, rr.t[:], ALU.mult, ALU.mult,
                             [ps[2 + j].b, T2.b, rr.b], [fin[2 + j].b])
                    self.cp("act", ckvT.t[:, j, kc0:kc0 + 512], fin[2 + j].t[:], [fin[2 + j].b], [ckvT.b])
                wr = load_wblk(l, 24)
                proj_block(wr, ps[2])
                self.cp("act", knf.t[:], ps[2].t[:], [ps[2].b], [knf.b])
                self.cp("dve", knb.t[:], ps[2].t[:], [ps[2].b], [knb.b])
                rope_apply(knf, knb, rb_bf, rope[2], rope[3], fin[4].t[:], [fin[4].b])
                self.cp("act", krT.t[:, kc0:kc0 + 512], fin[4].t[:], [fin[4].b], [krT.b])
                self.chk("kr")
                wv0 = load_wblk(l, 25)
                wv1 = load_wblk(l, 26)
                for tt in range(4):
                    t = 4 * c + tt
                    pv = ps[6]
                    items = []
                    for half, wv in enumerate((wv0, wv1)):
                        for k in range(NKC):
                            items.append((pv.t[:, half * 128:(half + 1) * 128], hT.t[:, k, tt * 128:(tt + 1) * 128], wv.t[:, k, :], k == 0, k == NKC - 1))
                    self.mm(items, [hT.b, wv0.b, wv1.b], [pv.b])
                    self.cp("act", vst.t[:, 2 + t, :], pv.t[:, 0:256], [pv.b], [vst.b])
                    self.cp("dve", ostage_v.t[:], pv.t[:, 0:256], [pv.b], [ostage_v.b])
                    self.dma("sp", onv_d[l, t * 128:(t + 1) * 128, :], ostage_v.t[:], reads=[ostage_v.b], writes=[B_out])
                    po = ps[7]
                    self.tr([(po.t[:, i * 128:(i + 1) * 128], fin[i].t[:, tt * 128:(tt + 1) * 128], ident_f.t[:]) for i in range(4)],
                            [fin[0].b, fin[1].b, fin[2].b, fin[3].b, ident_f.b], [po.b])
                    og = ostage[tt % 2]
                    self.cp("act", og.t[:], po.t[:], [po.b], [og.b])
                    self.dma("sp", onk_d[l, t * 128:(t + 1) * 128, :], og.t[:, 0:256], reads=[og.b], writes=[B_out])
                    self.dma("sp", onc_d[l, t * 128:(t + 1) * 128, :], og.t[:, 256:512], reads=[og.b], writes=[B_out])
                    self.tr([(po.t[:, 0:128], fin[4].t[:, tt * 128:(tt + 1) * 128], ident_f.t[:])], [fin[4].b, ident_f.b], [po.b])
                    self.cp("dve", ostage_r.t[:], po.t[:, 0:128], [po.b], [ostage_r.b])
                    self.dma("sp", onr_d[l, t * 128:(t + 1) * 128, :], ostage_r.t[:, 0:64], reads=[ostage_r.b], writes=[B_out])

            self.chk("passA")
            for h in range(8):
                for g in range(5):
                    n0 = g * 512
                    nn = 512 if g < 4 else 256
                    pb = ps[2 + g % 2]
                    self.mmg(pb.t[:, 0:nn], [(wuk.t[:, j, h * 128:(h + 1) * 128], ckvT.t[:, j, n0:n0 + nn]) for j in range(2)],
                             [wuk.b, ckvT.b], [pb.b])
                    self.cp("act" if g % 2 == 0 else "dve", knx.t[:, n0:n0 + nn], pb.t[:, 0:nn], [pb.b], [knx.b])
                self.dma("sp", knscr_d[h], knx.t[:], reads=[knx.b], writes=[B_knscr])
                for g in range(5):
                    kb0 = g * 4
                    nk = 4 if g < 4 else 2
                    pb = ps[2 + g % 2]
                    items = []
                    for i in range(nk):
                        kb = kb0 + i
                        for j in range(2):
                            items.append((pb.t[:, i * 128:(i + 1) * 128], ckvT.t[:, j, kb * 128:(kb + 1) * 128], wuv.t[:, j, h * 128:(h + 1) * 128], j == 0, j == 1))
                    self.mm(items, [wuv.b, ckvT.b], [pb.b])
                    self.cp("act" if g % 2 == 0 else "dve", vbx.t[:, kb0 * 128:(kb0 + nk) * 128], pb.t[:, 0:nk * 128], [pb.b], [vbx.b])
                self.dma("sp", vbscr_d[h], vbx.t[:], reads=[vbx.b], writes=[B_vbscr])
            self.S.barrier()

            self.chk("expand")
            self.dma("sp", gtab.t[:], gscr_d[l, 0:1, :].to_broadcast([128, D]), reads=[B_gscr], writes=[gtab.b])
            for c in range(4):
                norm_chunk(l, c, 0)
                load_rope(c)
                def gqa_stages(h):
                    q_ = qT[h % 2]
                    st = {}

                    def s0():
                        w = load_wblk(l, h)
                        proj_block(w, ps[2])

                    def s1():
                        self.act(sq.t[:], ps[2].t[:], AF.Square, [ps[2].b], [sq.b])

                    def s2():
                        rstd_mm([sq], ps[3])

                    def s3():
                        rstd_act(128.0, ps[3])

                    def s4():
                        self.stt("dve", knf.t[:], ps[2].t[:], qg, rr.t[:], ALU.mult, ALU.mult, [ps[2].b, T2.b, rr.b], [knf.b])
                        self.cp("act", knb.t[:], knf.t[:], [knf.b], [knb.b])

                    def s5():
                        rope_mm(knb, ra_bf, ps[3])

                    def s6():
                        rope_fin(knf, rope[0], rope[1], q_.t[:], [q_.b], ps[3])

                    return [s0, s1, s2, s3, s4, s5, s6]

                def gqa_attn(h, inject):
                    q_ = qT[h % 2]
                    kv = h // 4
                    attention(c, h,
                              [((lambda kb, kv_=kv: kT.t[:, kv_, kb * 128:(kb + 1) * 128]), q_.t[:])],
                              (lambda kb, kv_=kv: vst.t[:, kb, kv_ * 128:(kv_ + 1) * 128]),
                              128.0 ** -0.5, [vst.b], [kT.b], [q_.b], inject)

                def mla_stages(h):
                    kn_, vb_ = knT[h % 2], vbh[h % 2]
                    qn_ = qnT[h % 2]
                    qr_ = qrT[(h // 2) % 2]
                    out = []

                    def d0():
                        self.dma("sp", kn_.t[:], knscr_d[h], reads=[B_knscr], writes=[kn_.b])
                        self.dma("sp", vb_.t[:], vbscr_d[h].rearrange("p (kb n) -> p kb n", n=128), reads=[B_vbscr], writes=[vb_.b])

                    if h % 2 == 0:
                        def r0():
                            w = load_wblk(l, 16 + h // 2)
                            proj_block(w, ps[3])

                        def r1():
                            self.cp("act", knf.t[:], ps[3].t[:], [ps[3].b], [knf.b])
                            self.cp("act", knb.t[:], knf.t[:], [knf.b], [knb.b])

                        def r2():
                            rope_mm(knb, rb_bf, ps[3])

                        def r3():
                            rope_fin(knf, rope[2], rope[3], qr_.t[:], [qr_.b], ps[3])

                        out += [r0, d0, r1, r2, r3]

                    def n0():
                        w = load_wblk(l, 8 + h)
                        proj_block(w, ps[2])

                    def n1():
                        self.cp("act", qn_.t[:], ps[2].t[:], [ps[2].b], [qn_.b])

                    if h % 2 == 0:
                        out += [n0, n1]
                    else:
                        out += [n0, d0, n1]
                    return out

                def mla_attn(h, inject):
                    kn_, vb_ = knT[h % 2], vbh[h % 2]
                    qn_ = qnT[h % 2]
                    qr_ = qrT[(h // 2) % 2]
                    r0_ = (h % 2) * 64
                    attention(c, 8 + h,
                              [((lambda kb: kn_.t[:, kb * 128:(kb + 1) * 128]), qn_.t[:]),
                               ((lambda kb: krT.t[r0_:r0_ + 64, kb * 128:(kb + 1) * 128]), qr_.t[r0_:r0_ + 64, :])],
                              (lambda kb: vb_.t[:, kb, :]),
                              192.0 ** -0.5, [vb_.b], [kn_.b, krT.b], [qn_.b, qr_.b], inject)

                heads = [("g", h) for h in range(8)] + [("m", h) for h in range(8)]

                def stages_of(i):
                    kind, h = heads[i]
                    return gqa_stages(h) if kind == "g" else mla_stages(h)

                for s_ in stages_of(0):
                    s_()
                for i in range(16):
                    inject = {}
                    if i + 1 < 16:
                        sl = stages_of(i + 1)
                        for j, s_ in enumerate(sl):
                            inject[2 * j] = s_
                    kind, h = heads[i]
                    if kind == "g":
                        gqa_attn(h, inject)
                    else:
                        mla_attn(h, inject)
                it2 = 0
                for nb in range(8):
                    wo_ = wo[nb % 2]
                    self.dma("pool", wo_.t[:], wo_d[l, :, nb * 256:(nb + 1) * 256].rearrange("(k p) c -> p k c", p=128), writes=[wo_.b])
                    for tt in range(4):
                        pb = ps[2 + it2 % 4]
                        self.mmg(pb.t[:, 0:256], [(oT[k].t[:, tt * 128:(tt + 1) * 128], wo_.t[:, k, :]) for k in range(NKC)],
                                 [o.b for o in oT] + [wo_.b], [pb.b])
                        resid_update(pb, 4 * c + tt, nb, 256, it2 % 2)
                        it2 += 1
            self.S.barrier()

            self.chk("passB")
            self.dma("sp", gtab.t[:], gscr_d[l, 1:2, :].to_broadcast([128, D]), reads=[B_gscr], writes=[gtab.b])
            for c in range(4):
                norm_chunk(l, c, 1)
                for fg in range(22):
                    wg_ = wgb[fg % 3]
                    wu_ = wub[fg % 3]
                    self.dma("pool", wg_.t[:], wg_d[l, :, fg * 256:(fg + 1) * 256].rearrange("(k p) c -> p k c", p=128), writes=[wg_.b])
                    self.dma("pool", wu_.t[:], wu_d[l, :, fg * 256:(fg + 1) * 256].rearrange("(k p) c -> p k c", p=128), writes=[wu_.b])
                    for sub in range(2):
                        fb = fg * 2 + sub
                        pa = ps[2 + fb % 2]
                        pu = ps[4 + fb % 2]
                        self.mmg(pa.t[:], [(wg_.t[:, k, sub * 128:(sub + 1) * 128], hT.t[:, k, :]) for k in range(NKC)], [wg_.b, hT.b], [pa.b])
                        self.mmg(pu.t[:], [(wu_.t[:, k, sub * 128:(sub + 1) * 128], hT.t[:, k, :]) for k in range(NKC)], [wu_.b, hT.b], [pu.b])
                        sg_ = sg[fb % 2]
                        self.act(sg_.t[:], pa.t[:], AF.Silu, [pa.b], [sg_.b])
                        self.tt("dve", gT[fb].t[:], sg_.t[:], pu.t[:], ALU.mult, [sg_.b, pu.b], [gT[fb].b])
                wdi = 0
                for nb in range(4):
                    for ks in range(11):
                        wd_ = wdb[wdi % 6]
                        wdi += 1
                        self.dma("pool", wd_.t[:], wd_d[l, ks * 512:(ks + 1) * 512, nb * 512:(nb + 1) * 512].rearrange("(j p) c -> p j c", p=128), writes=[wd_.b])
                        items = []
                        for tt in range(4):
                            for j in range(4):
                                f = ks * 4 + j
                                items.append((ps[4 + tt].t[:], gT[f].t[:, tt * 128:(tt + 1) * 128], wd_.t[:, j, :], f == 0, f == NFC - 1))
                        self.mm(items, [gT[ks * 4 + j].b for j in range(4)] + [wd_.b], [ps[4 + tt].b for tt in range(4)])
                    for tt in range(4):
                        resid_update(ps[4 + tt], 4 * c + tt, nb, 512, tt % 2)
            self.S.barrier()

        self.dma("sp", gtab.t[:], nfin_d.rearrange("(o d) -> o d", o=1).to_broadcast([128, D]), writes=[gtab.b])
        for t in range(16):
            ss_, rs_ = ssb[t % 2], rsb[t % 2]
            self.dma("sp", xt.t[:], xres_d[t * 128:(t + 1) * 128, :], reads=[B_xres], writes=[xt.b])
            rms_stats(xt, ss_, rs_, float(D), xn[t % 2])
            self.stt("dve", xt.t[:], xt.t[:], rs_.t[:, 0:1], gtab.t[:], ALU.mult, ALU.mult, [xt.b, rs_.b, gtab.b], [xt.b])
            self.dma("sp", y_d[t * 128:(t + 1) * 128, :], xt.t[:], reads=[xt.b], writes=[B_out])


def _rope_tables(prompt):
    def tables(dim):
        nf = dim // 4
        half = dim // 2
        if prompt:
            return np.ones((dim, T), np.float32), np.zeros((dim, T), np.float32)
        tpos = np.arange(T)
        row = (tpos // 64).astype(np.float32)
        col = (tpos % 64).astype(np.float32)
        inv = (np.float32(10000.0) ** (-np.arange(nf, dtype=np.float32) / np.float32(nf))).astype(np.float32)
        ang = np.concatenate([row[:, None] * inv[None, :], col[:, None] * inv[None, :]], axis=-1)
        cs, sn = np.cos(ang).astype(np.float32), np.sin(ang).astype(np.float32)
        idx = np.arange(dim) % half
        return np.ascontiguousarray(cs[:, idx].T), np.ascontiguousarray(sn[:, idx].T)

    ca, sa = tables(128)
    cb, sb_ = tables(64)
    cb = np.concatenate([cb, cb], axis=0)
    sb_ = np.concatenate([sb_, sb_], axis=0)
    return ca, sa, cb, sb_


def _rot_mats():
    ra = np.zeros((128, 128), np.float32)
    for d in range(64):
        ra[d + 64, d] = -1.0
        ra[d, d + 64] = 1.0
    rb = np.zeros((128, 128), np.float32)
    for blk in range(2):
        o = blk * 64
        for e in range(32):
            rb[o + e + 32, o + e] = -1.0
            rb[o + e, o + e + 32] = 1.0
    return ra, rb


def _masks(prompt):
    mA = np.zeros((9, NKEY), np.float32)
    mB = np.zeros((9, T), np.float32)
    if prompt:
        mA[8, 0:256] = 1.0
        for s in range(8):
            mA[s, 256 + s * 256:256 + (s + 1) * 256] = 1.0
        mB[:, :] = NEG
        for s in range(8):
            mB[s, s * 256:(s + 1) * 256] = 0.0
    return mA, mB


def _permute_w_in(w_in):
    qa = w_in[:, :, 0:1024]
    ka = w_in[:, :, 1024:1280]
    va = w_in[:, :, 1280:1536]
    nl_ = w_in.shape[0]
    qb = w_in[:, :, 1536:3072].reshape(nl_, D, 8, 192)
    qbn = qb[:, :, :, 0:128].reshape(nl_, D, 1024)
    qbr = qb[:, :, :, 128:192].reshape(nl_, D, 512)
    ckv = w_in[:, :, 3072:3328]
    kr = w_in[:, :, 3328:3392]
    return np.ascontiguousarray(np.concatenate([qa, qbn, qbr, ka, ckv, kr, kr, va], axis=2))


_NC_CACHE = {}


def _get_nc(n_layers=L):
    if n_layers not in _NC_CACHE:
        _NC_CACHE[n_layers] = MK(n_layers).build()
    return _NC_CACHE[n_layers]


def make_in_maps(inp, nl=L):
    f = lambda a: np.ascontiguousarray(np.asarray(a, dtype=np.float32))
    fl = lambda a: np.ascontiguousarray(np.asarray(a, dtype=np.float32)[:nl])
    shared = {
        "w_ada": fl(inp["w_ada"]), "b_ada": fl(inp["b_ada"]), "norm_attn": fl(inp["norm_attn"]),
        "norm_ffn": fl(inp["norm_ffn"]), "w_in": _permute_w_in(fl(inp["w_in"])), "qnorm": fl(inp["qnorm_a"]),
        "knorm": fl(inp["knorm_a"]), "kvnorm": fl(inp["kvnorm_b"]), "w_uk": fl(inp["w_uk_b"]), "w_uv": fl(inp["w_uv_b"]),
        "w_o": fl(inp["w_o"]), "w_gate": fl(inp["w_gate"]), "w_up": fl(inp["w_up"]), "w_down": fl(inp["w_down"]),
        "norm_final": f(inp["norm_final"]), "ident": np.eye(128, dtype=np.float32),
    }
    ra, rb = _rot_mats()
    shared["rotA"], shared["rotB"] = ra, rb
    xp = f(inp["x_prompt"])
    xsm = f(inp["x_sample"])
    ck, cv = f(inp["cache_k_a"]), f(inp["cache_v_a"])
    cc, cr = f(inp["cache_ckv_b"]), f(inp["cache_krope_b"])
    cvec_s, cvec_p = f(inp["c"]), f(inp["c_ctx"])
    tabs = {False: _rope_tables(False), True: _rope_tables(True)}
    msk = {False: _masks(False), True: _masks(True)}
    in_maps = []
    for core in range(8):
        prompt = core >= 4
        m = dict(shared)
        if prompt:
            i = core - 4
            m["x"] = np.ascontiguousarray(xp[i * 8:(i + 1) * 8].reshape(T, D))
            m["cvec"] = cvec_p
            m["cache_k"] = np.zeros((nl, 256, 256), np.float32)
            m["cache_v"] = np.zeros((nl, 256, 256), np.float32)
            m["cache_c"] = np.zeros((nl, 256, 256), np.float32)
            m["cache_r"] = np.zeros((nl, 256, 64), np.float32)
        else:
            b = core
            m["x"] = np.ascontiguousarray(xsm[b])
            m["cvec"] = np.ascontiguousarray(cvec_s[b])
            m["cache_k"] = np.ascontiguousarray(ck[b].reshape(L, 256, 256)[:nl])
            m["cache_v"] = np.ascontiguousarray(cv[b].reshape(L, 256, 256)[:nl])
            m["cache_c"] = np.ascontiguousarray(cc[b][:nl])
            m["cache_r"] = np.ascontiguousarray(cr[b][:nl])
        ca, sa, cb, sb_ = tabs[prompt]
        m["cosA"], m["sinA"], m["cosB"], m["sinB"] = ca, sa, cb, sb_
        m["maskA"], m["maskB"] = msk[prompt]
        in_maps.append(m)
    return in_maps


def assemble(results):
    y_sample = np.stack([results[b]["y"] for b in range(4)], axis=0)
    y_prompt = np.concatenate([results[4 + i]["y"].reshape(8, 256, D) for i in range(4)], axis=0)

    def gather(name, tail):
        parts = []
        for i in range(4):
            a = results[4 + i][name]
            a = a.reshape(L, 8, 256, *tail).transpose(1, 0, 2, *range(3, 3 + len(tail)))
            parts.append(a)
        return np.ascontiguousarray(np.concatenate(parts, axis=0))

    new_k = gather("onk", (2, 128))
    new_v = gather("onv", (2, 128))
    new_c = gather("onc", (256,))
    new_r = gather("onr", (64,))
    return (y_prompt.astype(np.float32), y_sample.astype(np.float32), new_k, new_v, new_c, new_r)


def kernel(**inputs):
    nc = _get_nc(L)
    in_maps = make_in_maps(inputs)
    res = run_bass_kernel_spmd(nc, in_maps, core_ids=list(range(8)))
    return assemble(res.results)
```

```python
import numpy as np
import contextlib
import concourse.bass as bass
import concourse.mybir as mybir
from concourse.bass_utils import run_bass_kernel_spmd

F32 = mybir.dt.float32
BF16 = mybir.dt.bfloat16
AF = mybir.ActivationFunctionType
ALU = mybir.AluOpType
AX = mybir.AxisListType

D = 2048
T = 2048
L = 4
DFF = 5632
NKC = 16
NFC = 44
NKEY = 2304
NKB = 18
WIN = 3456
EPS = 1e-6
NEG = -30000.0

ENGS = ("pe", "act", "dve", "pool", "sp")
NDMASEM = 16


class Buf:
    __slots__ = ("name", "last_w", "readers", "dma_readers", "excl")

    def __init__(self, name):
        self.excl = False
        self.name = name
        self.last_w = None
        self.readers = {}
        self.dma_readers = []


class Op:
    __slots__ = ("eng", "fn", "deps", "is_dma", "sem", "val", "needs_sig")

    def __init__(self, eng, fn, is_dma):
        self.eng = eng
        self.fn = fn
        self.deps = set()
        self.is_dma = is_dma
        self.sem = None
        self.val = 0
        self.needs_sig = False


class Sched:
    def __init__(self):
        self.streams = {e: [] for e in ENGS}
        self.ndma = {e: 0 for e in ENGS}
        self.dma_hist = {e: [] for e in ENGS}
        self.bar_deps = {}

    def add(self, eng, fn, reads=(), writes=(), dma=False):
        op = Op(eng, fn, dma)
        deps = op.deps
        bd = self.bar_deps.get(eng)
        if bd:
            deps.update(bd)
            self.bar_deps[eng] = None
        for b in reads:
            if b.last_w is not None:
                deps.add(b.last_w)
            if b.excl:
                for e2, r in b.readers.items():
                    if e2 != eng:
                        deps.add(r)
        for b in writes:
            if b.last_w is not None:
                deps.add(b.last_w)
            for r in b.readers.values():
                deps.add(r)
            for r in b.dma_readers:
                deps.add(r)
        for b in reads:
            if dma:
                b.dma_readers.append(op)
            else:
                b.readers[eng] = op
        for b in writes:
            b.last_w = op
            b.readers = {}
            b.dma_readers = []
        deps.discard(op)
        if dma:
            j = self.ndma[eng]
            self.ndma[eng] = j + 1
            op.sem = ("dma", eng, j % NDMASEM)
            op.val = 16 * (j // NDMASEM + 1)
            hist = self.dma_hist[eng]
            if j >= NDMASEM:
                deps.add(hist[j - NDMASEM])
            hist.append(op)
        self.streams[eng].append(op)
        return op

    def barrier(self):
        deps = set()
        for e in ENGS:
            comp = [op for op in self.streams[e][-64:] if not op.is_dma]
            if comp:
                deps.add(comp[-1])
            else:
                comp = [op for op in self.streams[e] if not op.is_dma]
                if comp:
                    deps.add(comp[-1])
            for d in self.dma_hist[e][-NDMASEM:]:
                deps.add(d)
        for e in ENGS:
            self.bar_deps[e] = set(deps) | (self.bar_deps.get(e) or set())

    def plan(self):
        for e in ENGS:
            for op in self.streams[e]:
                for d in op.deps:
                    if not d.is_dma:
                        d.needs_sig = True
        for e in ENGS:
            cnt = 0
            for op in self.streams[e]:
                if op.is_dma:
                    continue
                if op.needs_sig:
                    cnt += 1
                    op.sem = ("prog", e)
                    op.val = cnt

    def emit(self, nc):
        self.plan()
        with contextlib.ExitStack() as es:
            sems = {}
            for e in ENGS:
                sems[("prog", e)] = es.enter_context(nc.semaphore("prog_" + e))
                for i in range(NDMASEM):
                    if self.ndma[e] > i:
                        sems[("dma", e, i)] = es.enter_context(nc.semaphore("dma_%s_%d" % (e, i)))
            block = es.enter_context(nc.Block())
            streams = self.streams
            dma_hist = self.dma_hist

            def run_stream(ename, eng):
                waited = {}
                for op in streams[ename]:
                    need = {}
                    for d in op.deps:
                        if need.get(d.sem, 0) < d.val:
                            need[d.sem] = d.val
                    for k, v in need.items():
                        if waited.get(k, 0) < v:
                            eng.wait_ge(sems[k], v)
                            waited[k] = v
                    ins = op.fn(eng)
                    if op.is_dma:
                        ins.then_inc(sems[op.sem], 16)
                    elif op.needs_sig:
                        ins.then_inc(sems[op.sem], 1)
                last = {}
                for op in dma_hist[ename]:
                    last[op.sem] = op.val
                for k, v in last.items():
                    if waited.get(k, 0) < v:
                        eng.wait_ge(sems[k], v)

            @block.tensor
            def _(eng):
                run_stream("pe", eng)

            @block.scalar
            def _(eng):
                run_stream("act", eng)

            @block.vector
            def _(eng):
                run_stream("dve", eng)

            @block.gpsimd
            def _(eng):
                run_stream("pool", eng)

            @block.sync
            def _(eng):
                run_stream("sp", eng)


class TB:
    __slots__ = ("t", "b")

    def __init__(self, t, name):
        self.t = t
        self.b = Buf(name)


_DT_SIZE = {F32: 4, BF16: 2}


class MK:
    def __init__(self, n_layers=L, stop=None, dbg=False):
        self.dbg = dbg
        self.stop = stop
        self.nl = n_layers
        self.nc = bass.Bass("TRN2", target_bir_lowering=False)
        self.S = Sched()
        self.off = 16512
        self.uid = 0

    def sb(self, name, shape, dt):
        n = 1
        for s in shape[1:]:
            n *= s
        nbytes = (n * _DT_SIZE[dt] + 63) // 64 * 64
        self.uid += 1
        t = self.nc.alloc_sbuf_tensor_at("%s_%d" % (name, self.uid), list(shape), dt, offset=self.off)
        self.off += nbytes
        assert self.off <= 229344, ("SBUF overflow", name, self.off)
        return TB(t, name)

    def din(self, name, shape, dt=F32):
        if self.dbg and name in ("w_ada", "w_gate", "w_up", "w_down", "w_o"):
            return self.nc.dram_tensor(name, [self.nl, 8, 8], dt, kind="ExternalInput").ap()
        return self.nc.dram_tensor(name, list(shape), dt, kind="ExternalInput").ap()

    def dout(self, name, shape, dt=F32):
        return self.nc.dram_tensor(name, list(shape), dt, kind="ExternalOutput").ap()

    def dma(self, q, out_ap, in_ap, reads=(), writes=()):
        self.S.add(q, lambda e: e.dma_start(out=out_ap, in_=in_ap), reads, writes, dma=True)

    def mm(self, items, reads, writes):
        def fn(e):
            ins = None
            for (o, l, r, st, sp) in items:
                ins = e.matmul(o, lhsT=l, rhs=r, start=st, stop=sp)
            return ins

        self.S.add("pe", fn, reads, writes)

    def mmg(self, out, pairs, reads, writes):
        n = len(pairs)
        self.mm([(out, l, r, i == 0, i == n - 1) for i, (l, r) in enumerate(pairs)], reads, writes)

    def tr(self, items, reads, writes):
        def fn(e):
            ins = None
            for (o, i, idn) in items:
                ins = e.transpose(o, i, idn)
            return ins

        self.S.add("pe", fn, reads, writes)

    def act(self, out, in_, func, R, W, scale=None, bias=None, accum=None):
        kw = {}
        if scale is not None:
            kw["scale"] = scale
        if bias is not None:
            kw["bias"] = bias
        if accum is not None:
            kw["accum_out"] = accum
        self.S.add("act", lambda e: e.activation(out=out, in_=in_, func=func, **kw), R, W)

    def cp(self, eng, out, in_, R, W):
        if eng == "act":
            self.S.add("act", lambda e: e.copy(out=out, in_=in_), R, W)
        else:
            self.S.add(eng, lambda e: e.tensor_copy(out=out, in_=in_), R, W)

    def tt(self, eng, out, in0, in1, op, R, W):
        self.S.add(eng, lambda e: e.tensor_tensor(out=out, in0=in0, in1=in1, op=op), R, W)

    def stt(self, eng, out, in0, scalar, in1, op0, op1, R, W):
        self.S.add(eng, lambda e: e.scalar_tensor_tensor(out=out, in0=in0, scalar=scalar, in1=in1, op0=op0, op1=op1), R, W)

    def ts1(self, eng, out, in0, scalar1, op0, R, W):
        self.S.add(eng, lambda e: e.tensor_scalar(out=out, in0=in0, scalar1=scalar1, scalar2=None, op0=op0), R, W)

    def memset(self, eng, ap, val, W):
        self.S.add(eng, lambda e: e.memset(ap, val), [], W)

    def recip(self, out, in_, R, W):
        self.S.add("dve", lambda e: e.reciprocal(out=out, in_=in_), R, W)

    def rsum(self, out, in_, R, W):
        self.S.add("dve", lambda e: e.reduce_sum(out=out, in_=in_, axis=AX.X), R, W)

    def build(self):
        try:
            self._build()
        except StopIteration:
            pass
        self.S.emit(self.nc)
        return self.nc

    def chk(self, tag):
        if self.stop == tag:
            raise StopIteration()

    def _build(self):
        nc = self.nc
        nl = self.nl
        x_d = self.din("x", [T, D])
        cvec_d = self.din("cvec", [D])
        wada_d = self.din("w_ada", [nl, D, 6 * D])
        bada_d = self.din("b_ada", [nl, 6 * D])
        nattn_d = self.din("norm_attn", [nl, D])
        nffn_d = self.din("norm_ffn", [nl, D])
        win_d = self.din("w_in", [nl, D, WIN])
        qn_d = self.din("qnorm", [nl, 128])
        kn_d = self.din("knorm", [nl, 128])
        kvn_d = self.din("kvnorm", [nl, 256])
        wuk_d = self.din("w_uk", [nl, 256, 1024])
        wuv_d = self.din("w_uv", [nl, 256, 1024])
        wo_d = self.din("w_o", [nl, D, D])
        wg_d = self.din("w_gate", [nl, D, DFF])
        wu_d = self.din("w_up", [nl, D, DFF])
        wd_d = self.din("w_down", [nl, DFF, D])
        nfin_d = self.din("norm_final", [D])
        ck_d = self.din("cache_k", [nl, 256, 256])
        cv_d = self.din("cache_v", [nl, 256, 256])
        cc_d = self.din("cache_c", [nl, 256, 256])
        cr_d = self.din("cache_r", [nl, 256, 64])
        cosA_d = self.din("cosA", [128, T])
        sinA_d = self.din("sinA", [128, T])
        cosB_d = self.din("cosB", [128, T])
        sinB_d = self.din("sinB", [128, T])
        ident_d = self.din("ident", [128, 128])
        ra_d = self.din("rotA", [128, 128])
        rb_d = self.din("rotB", [128, 128])
        mA_d = self.din("maskA", [9, NKEY])
        mB_d = self.din("maskB", [9, T])

        y_d = self.dout("y", [T, D])
        onk_d = self.dout("onk", [L, T, 256])
        onv_d = self.dout("onv", [L, T, 256])
        onc_d = self.dout("onc", [L, T, 256])
        onr_d = self.dout("onr", [L, T, 64])
        xres_d = self.dout("xres", [T, D])
        gscr_d = nc.dram_tensor("gscr", [L, 2, D], F32).ap()
        knscr_d = nc.dram_tensor("knscr", [8, 128, NKEY], BF16).ap()
        vbscr_d = nc.dram_tensor("vbscr", [8, 128, NKB * 128], BF16).ap()
        B_xres = Buf("xres")
        B_gscr = Buf("gscr")
        B_knscr = Buf("knscr")
        B_vbscr = Buf("vbscr")
        B_out = Buf("outs")

        ps = [TB(nc.alloc_psum_tensor("ps%d" % i, [128, 1024], BF16), "ps%d" % i) for i in range(2)]
        ps += [TB(nc.alloc_psum_tensor("ps%d" % i, [128, 512], F32), "ps%d" % i) for i in range(2, 8)]

        for p_ in ps:
            p_.b.excl = True
        ident_bf = self.sb("ident_bf", [128, 128], BF16)
        ident_f = self.sb("ident_f", [128, 128], F32)
        ones_bf = self.sb("ones_bf", [128, 128], BF16)
        ra_bf = self.sb("ra_bf", [128, 128], BF16)
        rb_bf = self.sb("rb_bf", [128, 128], BF16)
        mA = self.sb("mA", [9, NKEY], BF16)
        mB = self.sb("mB", [9, T], BF16)
        epsb = self.sb("epsb", [128, 1], F32)
        T1 = self.sb("T1", [128, 128], F32)
        T2 = self.sb("T2", [128, 32], F32)
        scol = self.sb("scol", [128, 16], F32)
        srep = self.sb("srep", [128, 16, 128], BF16)
        mcols = self.sb("mcols", [128, L * 4 * 16], F32)
        acols = self.sb("acols", [128, L * 2 * 16], F32)
        gtab = self.sb("gtab", [128, D], F32)
        xtl = [self.sb("xt%d" % i, [128, D], F32) for i in range(2)]
        xt = xtl[0]
        xn = [self.sb("xn%d" % i, [128, D], BF16) for i in range(2)]
        ssb = [self.sb("ss%d" % i, [128, 1], F32) for i in range(2)]
        rsb = [self.sb("rs%d" % i, [128, 1], F32) for i in range(2)]
        hT = self.sb("hT", [128, NKC, 512], BF16)
        xs = [self.sb("xs%d" % i, [128, 512], F32) for i in range(2)]
        xtmp = [self.sb("xtmp%d" % i, [128, 512], F32) for i in range(2)]
        st1 = self.sb("st1", [128, 128], F32)
        st2 = self.sb("st2", [32, 128], F32)
        work0 = self.off

        def mc(l, j):
            return mcols.t[:, (l * 4 + j) * 16:(l * 4 + j + 1) * 16]

        def ac(l, j):
            return acols.t[:, (l * 2 + j) * 16:(l * 2 + j + 1) * 16]

        self.dma("pool", ident_bf.t[:], ident_d, writes=[ident_bf.b])
        self.dma("pool", ra_bf.t[:], ra_d, writes=[ra_bf.b])
        self.dma("pool", rb_bf.t[:], rb_d, writes=[rb_bf.b])
        self.dma("pool", mA.t[:], mA_d, writes=[mA.b])
        self.dma("pool", mB.t[:], mB_d, writes=[mB.b])
        self.dma("sp", ident_f.t[:], ident_d, writes=[ident_f.b])
        self.memset("dve", ones_bf.t[:], 1.0, [ones_bf.b])
        self.memset("dve", epsb.t[:], EPS, [epsb.b])
        for i in range(4):
            self.dma("sp", xres_d[i * 512:(i + 1) * 512, :], x_d[i * 512:(i + 1) * 512, :], writes=[B_xres])
        self.memset("dve", st1.t[:], 0.0, [st1.b])
        self.memset("dve", st2.t[:], 0.0, [st2.b])
        self.dma("sp", st1.t[0:16 * nl, :], nattn_d.rearrange("l (k p) -> (l k) p", p=128), writes=[st1.b])
        self.dma("sp", st1.t[64:64 + 16 * nl, :], nffn_d.rearrange("l (k p) -> (l k) p", p=128), writes=[st1.b])
        self.dma("sp", st2.t[0:16, :], cvec_d.rearrange("(k p) -> k p", p=128), writes=[st2.b])
        self.dma("sp", st2.t[16:16 + nl, :], qn_d, writes=[st2.b])
        self.dma("sp", st2.t[20:20 + nl, :], kn_d, writes=[st2.b])
        self.dma("sp", st2.t[24:24 + 2 * nl, :], kvn_d.rearrange("l (j p) -> (l j) p", p=128), writes=[st2.b])
        self.tr([(ps[4].t[:, 0:128], st1.t[:], ident_f.t[:])], [st1.b, ident_f.b], [ps[4].b])
        self.cp("dve", T1.t[:], ps[4].t[:, 0:128], [ps[4].b], [T1.b])
        self.tr([(ps[5].t[:, 0:32], st2.t[:], ident_f.t[0:32, 0:32])], [st2.b, ident_f.b], [ps[5].b])
        self.cp("dve", T2.t[:], ps[5].t[:, 0:32], [ps[5].b], [T2.b])
        self.act(scol.t[:], T2.t[:, 0:16], AF.Silu, [T2.b], [scol.b])
        self.cp("dve", srep.t[:], scol.t[:].unsqueeze(2).to_broadcast([128, 16, 128]), [scol.b], [srep.b])

        self.chk("setup")
        self.off = work0
        wada = [self.sb("wada%d" % i, [128, NKC, 512], BF16) for i in range(2)]
        bbc = [self.sb("bbc%d" % i, [128, 512], F32) for i in range(2)]
        modt = [self.sb("modt%d" % i, [128, 512], F32) for i in range(2)]
        dtmp = [self.sb("dtmp%d" % i, [128, 128], F32) for i in range(2)]
        it = 0
        if self.dbg:
            self.memset("dve", mcols.t[:], 0.0, [mcols.b])
            self.memset("dve", acols.t[:], 1.0, [acols.b])
        for l in range(0 if self.dbg else nl):
            for blk in range(24):
                i2 = it % 2
                it += 1
                c0 = blk * 512
                self.dma("pool", wada[i2].t[:], wada_d[l, :, c0:c0 + 512].rearrange("(k p) c -> p k c", p=128),
                         writes=[wada[i2].b])
                self.dma("sp", bbc[i2].t[:], bada_d[l:l + 1, c0:c0 + 512].to_broadcast([128, 512]), writes=[bbc[i2].b])
                pb = ps[2 + i2]
                self.mmg(pb.t[:], [(srep.t[:, k, :], wada[i2].t[:, k, :]) for k in range(NKC)],
                         [srep.b, wada[i2].b], [pb.b])
                self.tt("dve", modt[i2].t[:], pb.t[:], bbc[i2].t[:], ALU.add, [pb.b, bbc[i2].b], [modt[i2].b])
                kind, q = blk // 4, blk % 4
                if kind in (2, 5):
                    self.dma("sp", gscr_d[l, kind // 3:kind // 3 + 1, q * 512:(q + 1) * 512], modt[i2].t[0:1, :],
                             reads=[modt[i2].b], writes=[B_gscr])
                else:
                    j = {0: 0, 1: 1, 3: 2, 4: 3}[kind]
                    for s in range(4):
                        kidx = q * 4 + s
                        dt_ = dtmp[s % 2]
                        self.tt("dve", dt_.t[:], modt[i2].t[:, s * 128:(s + 1) * 128], ident_f.t[:], ALU.mult,
                                [modt[i2].b, ident_f.b], [dt_.b])
                        self.rsum(mc(l, j)[:, kidx:kidx + 1], dt_.t[:], [dt_.b], [mcols.b])
            for j in range(2):
                self.stt("dve", ac(l, j), mc(l, 1 + 2 * j), 1.0, T1.t[:, j * 64 + l * 16:j * 64 + (l + 1) * 16],
                         ALU.add, ALU.mult, [mcols.b, T1.b], [acols.b])
        self.S.barrier()
        self.chk("mod")

        self.off = work0
        kT = self.sb("kT", [128, 2, NKEY], BF16)
        vst = self.sb("vst", [128, NKB, 256], BF16)
        krT = self.sb("krT", [128, NKEY], BF16)
        wblk = [self.sb("wblk%d" % i, [128, NKC, 128], BF16) for i in range(3)]
        rope = [self.sb("rope%d" % i, [128, 512], F32) for i in range(4)]
        sq = self.sb("sq", [128, 512], BF16)
        sq2 = self.sb("sq2", [128, 512], BF16)
        rr = self.sb("rr", [128, 512], F32)
        knf = self.sb("knf", [128, 512], F32)
        knb = self.sb("knb", [128, 512], BF16)
        t1 = self.sb("t1", [128, 512], F32)
        t2 = self.sb("t2", [128, 512], F32)
        fin = [self.sb("fin%d" % i, [128, 512], F32) for i in range(5)]
        ostage = [self.sb("ostage%d" % i, [128, 512], F32) for i in range(2)]
        ostage_r = self.sb("ostage_r", [128, 128], F32)
        ostage_v = self.sb("ostage_v", [128, 256], F32)
        cst = self.sb("cst", [128, 2, 256], F32)
        cst_r = self.sb("cst_r", [128, 2, 128], F32)
        abmark = self.off
        ckvT = self.sb("ckvT", [128, 2, NKEY], BF16)
        wuk = self.sb("wuk", [128, 2, 1024], BF16)
        wuv = self.sb("wuv", [128, 2, 1024], BF16)
        knx = self.sb("knx", [128, NKEY], BF16)
        vbx = self.sb("vbx", [128, NKB * 128], BF16)
        a_end = self.off
        self.off = abmark
        knT = [self.sb("knT%d" % i, [128, NKEY], BF16) for i in range(2)]
        vbh = [self.sb("vbh%d" % i, [128, NKB, 128], BF16) for i in range(2)]
        qT = [self.sb("qT%d" % i, [128, 512], BF16) for i in range(2)]
        qnT = [self.sb("qnT%d" % i, [128, 512], BF16) for i in range(2)]
        qrT = [self.sb("qrT%d" % i, [128, 512], BF16) for i in range(2)]
        pT = [self.sb("pT%d" % i, [128, 512], BF16) for i in range(3)]
        rinv = self.sb("rinv", [128, 512], F32)
        oT = [self.sb("oT%d" % i, [128, 512], BF16) for i in range(16)]
        wo = [self.sb("wo%d" % i, [128, NKC, 256], BF16) for i in range(2)]
        b_end = self.off
        self.off = work0
        gT = [self.sb("gT%d" % i, [128, 512], BF16) for i in range(NFC)]
        wgb = [self.sb("wg%d" % i, [128, NKC, 256], BF16) for i in range(3)]
        wub = [self.sb("wu%d" % i, [128, NKC, 256], BF16) for i in range(3)]
        wdb = [self.sb("wd%d" % i, [128, 4, 512], BF16) for i in range(6)]
        sg = [self.sb("sg%d" % i, [128, 512], F32) for i in range(2)]
        c_end = self.off
        self.off = max(a_end, b_end, c_end)
        self.sbuf_used = self.off

        def rms_stats(src, ss_, rs_, n, junk):
            self.memset("dve", ss_.t[:], 0.0, [ss_.b])
            self.act(junk.t[:], src.t[:], AF.Square, [src.b, ss_.b], [ss_.b, junk.b], accum=ss_.t[:])
            self.act(rs_.t[:], ss_.t[:], AF.Ln, [ss_.b, epsb.b], [rs_.b], scale=1.0 / n, bias=epsb.t[:])
            self.act(rs_.t[:], rs_.t[:], AF.Exp, [rs_.b], [rs_.b], scale=-0.5)

        def norm_chunk(l, c, j):
            A = ac(l, j)
            sh = mc(l, 2 * j)
            for tt in range(4):
                t = 4 * c + tt
                xnb = xn[tt % 2]
                xt = xtl[tt % 2]
                ss_, rs_ = ssb[tt % 2], rsb[tt % 2]
                self.dma("sp", xt.t[:], xres_d[t * 128:(t + 1) * 128, :], reads=[B_xres], writes=[xt.b])
                rms_stats(xt, ss_, rs_, float(D), xnb)
                self.ts1("dve", xnb.t[:], xt.t[:], rs_.t[:, 0:1], ALU.mult, [xt.b, rs_.b], [xnb.b])
                for half in range(2):
                    pb = ps[half]
                    self.tr([(pb.t[:, jj * 128:(jj + 1) * 128], xnb.t[:, (half * 8 + jj) * 128:(half * 8 + jj + 1) * 128], ident_bf.t[:])
                             for jj in range(8)], [xnb.b, ident_bf.b], [pb.b])
                    for jj in range(8):
                        k = half * 8 + jj
                        self.act(hT.t[:, k, tt * 128:(tt + 1) * 128], pb.t[:, jj * 128:(jj + 1) * 128], AF.Identity,
                                 [pb.b, acols.b, mcols.b], [hT.b], scale=A[:, k:k + 1], bias=sh[:, k:k + 1])

        wblk_i = [0]

        def load_wblk(l, blk):
            w = wblk[wblk_i[0] % len(wblk)]
            wblk_i[0] += 1
            self.dma("pool", w.t[:], win_d[l, :, blk * 128:(blk + 1) * 128].rearrange("(k p) c -> p k c", p=128),
                     writes=[w.b])
            return w

        def proj_block(w, pb):
            self.mmg(pb.t[:], [(w.t[:, k, :], hT.t[:, k, :]) for k in range(NKC)], [w.b, hT.b], [pb.b])

        def rstd_mm(sq_list, pbk):
            self.mmg(pbk.t[:], [(ones_bf.t[:], s.t[:]) for s in sq_list], [ones_bf.b] + [s.b for s in sq_list], [pbk.b])

        def rstd_act(n, pbk):
            self.act(rr.t[:], pbk.t[:], AF.Ln, [pbk.b, epsb.b], [rr.b], scale=1.0 / n, bias=epsb.t[:])
            self.act(rr.t[:], rr.t[:], AF.Exp, [rr.b], [rr.b], scale=-0.5)

        def rstd_from_sq(sq_list, n, pbk=None):
            pbk = pbk or ps[4]
            rstd_mm(sq_list, pbk)
            rstd_act(n, pbk)

        def rope_mm(src_b, rot_bf, pbk):
            self.mmg(pbk.t[:], [(rot_bf.t[:], src_b.t[:])], [rot_bf.b, src_b.b], [pbk.b])

        def rope_fin(src_f, cos_t, sin_t, out_ap, out_bufs, pbk):
            self.tt("dve", t1.t[:], src_f.t[:], cos_t.t[:], ALU.mult, [src_f.b, cos_t.b], [t1.b])
            self.tt("dve", t2.t[:], pbk.t[:], sin_t.t[:], ALU.mult, [pbk.b, sin_t.b], [t2.b])
            self.tt("dve", out_ap, t1.t[:], t2.t[:], ALU.add, [t1.b, t2.b], out_bufs)

        def rope_apply(src_f, src_b, rot_bf, cos_t, sin_t, out_ap, out_bufs, pbk=None):
            pbk = pbk or ps[5]
            rope_mm(src_b, rot_bf, pbk)
            rope_fin(src_f, cos_t, sin_t, out_ap, out_bufs, pbk)

        def load_rope(c):
            for i, d_ in enumerate((cosA_d, sinA_d, cosB_d, sinB_d)):
                self.dma("sp", rope[i].t[:], d_[:, c * 512:(c + 1) * 512], writes=[rope[i].b])

        def resid_update(pb, t, nb, ncols, i2):
            c0 = nb * ncols
            xs_, xm_ = xs[i2], xtmp[i2]
            self.dma("sp", xs_.t[:, 0:ncols], xres_d[t * 128:(t + 1) * 128, c0:c0 + ncols], reads=[B_xres], writes=[xs_.b])
            self.tt("dve", xm_.t[:, 0:ncols], pb.t[:, 0:ncols], gtab.t[:, c0:c0 + ncols], ALU.mult, [pb.b, gtab.b], [xm_.b])
            self.tt("dve", xs_.t[:, 0:ncols], xs_.t[:, 0:ncols], xm_.t[:, 0:ncols], ALU.add, [xs_.b, xm_.b], [xs_.b])
            self.dma("sp", xres_d[t * 128:(t + 1) * 128, c0:c0 + ncols], xs_.t[:, 0:ncols], reads=[xs_.b], writes=[B_xres])

        def attention(c, hidx, score_parts, v_of_kb, scale, v_bufs, k_bufs, q_bufs, inject):
            pO, pL = ps[6], ps[7]

            def emitS(kb):
                pS = ps[4 + kb % 2]
                pairs = [(lk(kb), rq) for (lk, rq) in score_parts]
                pairs.append((mA.t[0:9, kb * 128:(kb + 1) * 128], mB.t[0:9, c * 512:(c + 1) * 512]))
                self.mmg(pS.t[:], pairs, k_bufs + q_bufs + [mA.b, mB.b], [pS.b])

            emitS(0)
            emitS(1)
            if 0 in inject:
                inject[0]()
            def emitPV(kb):
                p_ = pT[kb % 3]
                self.mm([(pO.t[:], v_of_kb(kb), p_.t[:], kb == 0, kb == NKB - 1),
                         (pL.t[:], ones_bf.t[:], p_.t[:], kb == 0, kb == NKB - 1)],
                        v_bufs + [ones_bf.b, p_.b], [pO.b, pL.b])

            for kb in range(NKB):
                if kb >= 1 and kb + 1 < NKB:
                    emitS(kb + 1)
                pS = ps[4 + kb % 2]
                p_ = pT[kb % 3]
                self.act(p_.t[:], pS.t[:], AF.Exp, [pS.b], [p_.b], scale=scale)
                if kb >= 1:
                    emitPV(kb - 1)
                if kb > 0 and kb in inject:
                    inject[kb]()
            emitPV(NKB - 1)
            self.recip(rinv.t[:], pL.t[:], [pL.b], [rinv.b])
            o_ = oT[hidx]
            self.tt("dve", o_.t[:], pO.t[:], rinv.t[:], ALU.mult, [pO.b, rinv.b], [o_.b])

        for l in range(nl):
            qg = T2.t[:, 16 + l:17 + l]
            kg = T2.t[:, 20 + l:21 + l]
            self.dma("pool", wuk.t[:], wuk_d[l].rearrange("(j p) n -> p j n", p=128), writes=[wuk.b])
            self.dma("pool", wuv.t[:], wuv_d[l].rearrange("(j p) n -> p j n", p=128), writes=[wuv.b])
            self.dma("pool", vst.t[:, 0:2, :], cv_d[l].rearrange("(kt p) n -> p kt n", p=128), writes=[vst.b])
            self.dma("sp", cst.t[:], ck_d[l].rearrange("(kt p) n -> p kt n", p=128), writes=[cst.b])
            self.tr([(ps[4].t[:, (h * 2 + kt) * 128:(h * 2 + kt + 1) * 128], cst.t[:, kt, h * 128:(h + 1) * 128], ident_f.t[:])
                     for h in range(2) for kt in range(2)], [cst.b, ident_f.b], [ps[4].b])
            self.cp("act", kT.t[:, :, 0:256], ps[4].t[:].rearrange("p (h n) -> p h n", h=2), [ps[4].b], [kT.b])
            self.dma("sp", cst.t[:], cc_d[l].rearrange("(kt p) n -> p kt n", p=128), writes=[cst.b])
            self.tr([(ps[4].t[:, (h * 2 + kt) * 128:(h * 2 + kt + 1) * 128], cst.t[:, kt, h * 128:(h + 1) * 128], ident_f.t[:])
                     for h in range(2) for kt in range(2)], [cst.b, ident_f.b], [ps[4].b])
            self.cp("act", ckvT.t[:, :, 0:256], ps[4].t[:].rearrange("p (h n) -> p h n", h=2), [ps[4].b], [ckvT.b])
            self.dma("sp", cst_r.t[:, :, 0:64], cr_d[l].rearrange("(kt p) n -> p kt n", p=128), writes=[cst_r.b])
            self.dma("sp", cst_r.t[:, :, 64:128], cr_d[l].rearrange("(kt p) n -> p kt n", p=128), writes=[cst_r.b])
            self.tr([(ps[5].t[:, kt * 128:(kt + 1) * 128], cst_r.t[:, kt, :], ident_f.t[:]) for kt in range(2)],
                    [cst_r.b, ident_f.b], [ps[5].b])
            self.cp("act", krT.t[:, 0:256], ps[5].t[:, 0:256], [ps[5].b], [krT.b])

            self.chk("loads")
            for c in range(4):
                norm_chunk(l, c, 0)
                self.chk("norm")
                load_rope(c)
                kc0 = 256 + c * 512
                for h in range(2):
                    w = load_wblk(l, 20 + h)
                    pb = ps[2 + h]
                    proj_block(w, pb)
                    self.act(sq.t[:], pb.t[:], AF.Square, [pb.b], [sq.b])
                    rstd_from_sq([sq], 128.0)
                    self.stt("dve", knf.t[:], pb.t[:], kg, rr.t[:], ALU.mult, ALU.mult, [pb.b, T2.b, rr.b], [knf.b])
                    self.cp("act", knb.t[:], knf.t[:], [knf.b], [knb.b])
                    rope_apply(knf, knb, ra_bf, rope[0], rope[1], fin[h].t[:], [fin[h].b])
                    self.cp("act", kT.t[:, h, kc0:kc0 + 512], fin[h].t[:], [fin[h].b], [kT.b])
                self.chk("ka")
                wc0 = load_wblk(l, 22)
                wc1 = load_wblk(l, 23)
                proj_block(wc0, ps[2])
                proj_block(wc1, ps[3])
                self.act(sq.t[:], ps[2].t[:], AF.Square, [ps[2].b], [sq.b])
                self.act(sq2.t[:], ps[3].t[:], AF.Square, [ps[3].b], [sq2.b])
                rstd_from_sq([sq, sq2], 256.0)
                for j in range(2):
                    cg = T2.t[:, 24 + l * 2 + j:25 + l * 2 + j]
                    self.stt("dve", fin[2 + j].t[:], ps[2 + j].t[:], cg, rr.t[:], ALU.mult, ALU.mult,
                             [ps[2 + j].b, T2.b, rr.b], [fin[2 + j].b])
                    self.cp("act", ckvT.t[:, j, kc0:kc0 + 512], fin[2 + j].t[:], [fin[2 + j].b], [ckvT.b])
                wr = load_wblk(l, 24)
                proj_block(wr, ps[2])
                self.cp("act", knf.t[:], ps[2].t[:], [ps[2].b], [knf.b])
                self.cp("dve", knb.t[:], ps[2].t[:], [ps[2].b], [knb.b])
                rope_apply(knf, knb, rb_bf, rope[2], rope[3], fin[4].t[:], [fin[4].b])
                self.cp("act", krT.t[:, kc0:kc0 + 512], fin[4].t[:], [fin[4].b], [krT.b])
                self.chk("kr")
                wv0 = load_wblk(l, 25)
                wv1 = load_wblk(l, 26)
                for tt in range(4):
                    t = 4 * c + tt
                    pv = ps[6]
                    items = []
                    for half, wv in enumerate((wv0, wv1)):
                        for k in range(NKC):
                            items.append((pv.t[:, half * 128:(half + 1) * 128], hT.t[:, k, tt * 128:(tt + 1) * 128], wv.t[:, k, :], k == 0, k == NKC - 1))
                    self.mm(items, [hT.b, wv0.b, wv1.b], [pv.b])
                    self.cp("act", vst.t[:, 2 + t, :], pv.t[:, 0:256], [pv.b], [vst.b])
                    self.cp("dve", ostage_v.t[:], pv.t[:, 0:256], [pv.b], [ostage_v.b])
                    self.dma("sp", onv_d[l, t * 128:(t + 1) * 128, :], ostage_v.t[:], reads=[ostage_v.b], writes=[B_out])
                    po = ps[7]
                    self.tr([(po.t[:, i * 128:(i + 1) * 128], fin[i].t[:, tt * 128:(tt + 1) * 128], ident_f.t[:]) for i in range(4)],
                            [fin[0].b, fin[1].b, fin[2].b, fin[3].b, ident_f.b], [po.b])
                    og = ostage[tt % 2]
                    self.cp("act", og.t[:], po.t[:], [po.b], [og.b])
                    self.dma("sp", onk_d[l, t * 128:(t + 1) * 128, :], og.t[:, 0:256], reads=[og.b], writes=[B_out])
                    self.dma("sp", onc_d[l, t * 128:(t + 1) * 128, :], og.t[:, 256:512], reads=[og.b], writes=[B_out])
                    self.tr([(po.t[:, 0:128], fin[4].t[:, tt * 128:(tt + 1) * 128], ident_f.t[:])], [fin[4].b, ident_f.b], [po.b])
                    self.cp("dve", ostage_r.t[:], po.t[:, 0:128], [po.b], [ostage_r.b])
                    self.dma("sp", onr_d[l, t * 128:(t + 1) * 128, :], ostage_r.t[:, 0:64], reads=[ostage_r.b], writes=[B_out])

            self.chk("passA")
            for h in range(8):
                for g in range(5):
                    n0 = g * 512
                    nn = 512 if g < 4 else 256
                    pb = ps[2 + g % 2]
                    self.mmg(pb.t[:, 0:nn], [(wuk.t[:, j, h * 128:(h + 1) * 128], ckvT.t[:, j, n0:n0 + nn]) for j in range(2)],
                             [wuk.b, ckvT.b], [pb.b])
                    self.cp("act" if g % 2 == 0 else "dve", knx.t[:, n0:n0 + nn], pb.t[:, 0:nn], [pb.b], [knx.b])
                self.dma("sp", knscr_d[h], knx.t[:], reads=[knx.b], writes=[B_knscr])
                for g in range(5):
                    kb0 = g * 4
                    nk = 4 if g < 4 else 2
                    pb = ps[2 + g % 2]
                    items = []
                    for i in range(nk):
                        kb = kb0 + i
                        for j in range(2):
                            items.append((pb.t[:, i * 128:(i + 1) * 128], ckvT.t[:, j, kb * 128:(kb + 1) * 128], wuv.t[:, j, h * 128:(h + 1) * 128], j == 0, j == 1))
                    self.mm(items, [wuv.b, ckvT.b], [pb.b])
                    self.cp("act" if g % 2 == 0 else "dve", vbx.t[:, kb0 * 128:(kb0 + nk) * 128], pb.t[:, 0:nk * 128], [pb.b], [vbx.b])
                self.dma("sp", vbscr_d[h], vbx.t[:], reads=[vbx.b], writes=[B_vbscr])
            self.S.barrier()

            self.chk("expand")
            self.dma("sp", gtab.t[:], gscr_d[l, 0:1, :].to_broadcast([128, D]), reads=[B_gscr], writes=[gtab.b])
            for c in range(4):
                norm_chunk(l, c, 0)
                load_rope(c)
                def gqa_stages(h):
                    q_ = qT[h % 2]
                    st = {}

                    def s0():
                        w = load_wblk(l, h)
                        proj_block(w, ps[2])

                    def s1():
                        self.act(sq.t[:], ps[2].t[:], AF.Square, [ps[2].b], [sq.b])

                    def s2():
                        rstd_mm([sq], ps[3])

                    def s3():
                        rstd_act(128.0, ps[3])

                    def s4():
                        self.stt("dve", knf.t[:], ps[2].t[:], qg, rr.t[:], ALU.mult, ALU.mult, [ps[2].b, T2.b, rr.b], [knf.b])
                        self.cp("act", knb.t[:], knf.t[:], [knf.b], [knb.b])

                    def s5():
                        rope_mm(knb, ra_bf, ps[3])

                    def s6():
                        rope_fin(knf, rope[0], rope[1], q_.t[:], [q_.b], ps[3])

                    return [s0, s1, s2, s3, s4, s5, s6]

                def gqa_attn(h, inject):
                    q_ = qT[h % 2]
                    kv = h // 4
                    attention(c, h,
                              [((lambda kb, kv_=kv: kT.t[:, kv_, kb * 128:(kb + 1) * 128]), q_.t[:])],
                              (lambda kb, kv_=kv: vst.t[:, kb, kv_ * 128:(kv_ + 1) * 128]),
                              128.0 ** -0.5, [vst.b], [kT.b], [q_.b], inject)

                def mla_stages(h):
                    kn_, vb_ = knT[h % 2], vbh[h % 2]
                    qn_ = qnT[h % 2]
                    qr_ = qrT[(h // 2) % 2]
                    out = []

                    def d0():
                        self.dma("sp", kn_.t[:], knscr_d[h], reads=[B_knscr], writes=[kn_.b])
                        self.dma("sp", vb_.t[:], vbscr_d[h].rearrange("p (kb n) -> p kb n", n=128), reads=[B_vbscr], writes=[vb_.b])

                    if h % 2 == 0:
                        def r0():
                            w = load_wblk(l, 16 + h // 2)
                            proj_block(w, ps[3])

                        def r1():
                            self.cp("act", knf.t[:], ps[3].t[:], [ps[3].b], [knf.b])
                            self.cp("act", knb.t[:], knf.t[:], [knf.b], [knb.b])

                        def r2():
                            rope_mm(knb, rb_bf, ps[3])

                        def r3():
                            rope_fin(knf, rope[2], rope[3], qr_.t[:], [qr_.b], ps[3])

                        out += [r0, d0, r1, r2, r3]

                    def n0():
                        w = load_wblk(l, 8 + h)
                        proj_block(w, ps[2])

                    def n1():
                        self.cp("act", qn_.t[:], ps[2].t[:], [ps[2].b], [qn_.b])

                    if h % 2 == 0:
                        out += [n0, n1]
                    else:
                        out += [n0, d0, n1]
                    return out

                def mla_attn(h, inject):
                    kn_, vb_ = knT[h % 2], vbh[h % 2]
                    qn_ = qnT[h % 2]
                    qr_ = qrT[(h // 2) % 2]
                    r0_ = (h % 2) * 64
                    attention(c, 8 + h,
                              [((lambda kb: kn_.t[:, kb * 128:(kb + 1) * 128]), qn_.t[:]),
                               ((lambda kb: krT.t[r0_:r0_ + 64, kb * 128:(kb + 1) * 128]), qr_.t[r0_:r0_ + 64, :])],
                              (lambda kb: vb_.t[:, kb, :]),
                              192.0 ** -0.5, [vb_.b], [kn_.b, krT.b], [qn_.b, qr_.b], inject)

                heads = [("g", h) for h in range(8)] + [("m", h) for h in range(8)]

                def stages_of(i):
                    kind, h = heads[i]
                    return gqa_stages(h) if kind == "g" else mla_stages(h)

                for s_ in stages_of(0):
                    s_()
                for i in range(16):
                    inject = {}
                    if i + 1 < 16:
                        sl = stages_of(i + 1)
                        for j, s_ in enumerate(sl):
                            inject[2 * j] = s_
                    kind, h = heads[i]
                    if kind == "g":
                        gqa_attn(h, inject)
                    else:
                        mla_attn(h, inject)
                it2 = 0
                for nb in range(8):
                    wo_ = wo[nb % 2]
                    self.dma("pool", wo_.t[:], wo_d[l, :, nb * 256:(nb + 1) * 256].rearrange("(k p) c -> p k c", p=128), writes=[wo_.b])
                    for tt in range(4):
                        pb = ps[2 + it2 % 4]
                        self.mmg(pb.t[:, 0:256], [(oT[k].t[:, tt * 128:(tt + 1) * 128], wo_.t[:, k, :]) for k in range(NKC)],
                                 [o.b for o in oT] + [wo_.b], [pb.b])
                        resid_update(pb, 4 * c + tt, nb, 256, it2 % 2)
                        it2 += 1
            self.S.barrier()

            self.chk("passB")
            self.dma("sp", gtab.t[:], gscr_d[l, 1:2, :].to_broadcast([128, D]), reads=[B_gscr], writes=[gtab.b])
            for c in range(4):
                norm_chunk(l, c, 1)
                for fg in range(22):
                    wg_ = wgb[fg % 3]
                    wu_ = wub[fg % 3]
                    self.dma("pool", wg_.t[:], wg_d[l, :, fg * 256:(fg + 1) * 256].rearrange("(k p) c -> p k c", p=128), writes=[wg_.b])
                    self.dma("pool", wu_.t[:], wu_d[l, :, fg * 256:(fg + 1) * 256].rearrange("(k p) c -> p k c", p=128), writes=[wu_.b])
                    for sub in range(2):
                        fb = fg * 2 + sub
                        pa = ps[2 + fb % 2]
                        pu = ps[4 + fb % 2]
                        self.mmg(pa.t[:], [(wg_.t[:, k, sub * 128:(sub + 1) * 128], hT.t[:, k, :]) for k in range(NKC)], [wg_.b, hT.b], [pa.b])
                        self.mmg(pu.t[:], [(wu_.t[:, k, sub * 128:(sub + 1) * 128], hT.t[:, k, :]) for k in range(NKC)], [wu_.b, hT.b], [pu.b])
                        sg_ = sg[fb % 2]
                        self.act(sg_.t[:], pa.t[:], AF.Silu, [pa.b], [sg_.b])
                        self.tt("dve", gT[fb].t[:], sg_.t[:], pu.t[:], ALU.mult, [sg_.b, pu.b], [gT[fb].b])
                wdi = 0
                for nb in range(4):
                    for ks in range(11):
                        wd_ = wdb[wdi % 6]
                        wdi += 1
                        self.dma("pool", wd_.t[:], wd_d[l, ks * 512:(ks + 1) * 512, nb * 512:(nb + 1) * 512].rearrange("(j p) c -> p j c", p=128), writes=[wd_.b])
                        items = []
                        for tt in range(4):
                            for j in range(4):
                                f = ks * 4 + j
                                items.append((ps[4 + tt].t[:], gT[f].t[:, tt * 128:(tt + 1) * 128], wd_.t[:, j, :], f == 0, f == NFC - 1))
                        self.mm(items, [gT[ks * 4 + j].b for j in range(4)] + [wd_.b], [ps[4 + tt].b for tt in range(4)])
                    for tt in range(4):
                        resid_update(ps[4 + tt], 4 * c + tt, nb, 512, tt % 2)
            self.S.barrier()

        self.dma("sp", gtab.t[:], nfin_d.rearrange("(o d) -> o d", o=1).to_broadcast([128, D]), writes=[gtab.b])
        for t in range(16):
            ss_, rs_ = ssb[t % 2], rsb[t % 2]
            self.dma("sp", xt.t[:], xres_d[t * 128:(t + 1) * 128, :], reads=[B_xres], writes=[xt.b])
            rms_stats(xt, ss_, rs_, float(D), xn[t % 2])
            self.stt("dve", xt.t[:], xt.t[:], rs_.t[:, 0:1], gtab.t[:], ALU.mult, ALU.mult, [xt.b, rs_.b, gtab.b], [xt.b])
            self.dma("sp", y_d[t * 128:(t + 1) * 128, :], xt.t[:], reads=[xt.b], writes=[B_out])


def _rope_tables(prompt):
    def tables(dim):
        nf = dim // 4
        half = dim // 2
        if prompt:
            return np.ones((dim, T), np.float32), np.zeros((dim, T), np.float32)
        tpos = np.arange(T)
        row = (tpos // 64).astype(np.float32)
        col = (tpos % 64).astype(np.float32)
        inv = (np.float32(10000.0) ** (-np.arange(nf, dtype=np.float32) / np.float32(nf))).astype(np.float32)
        ang = np.concatenate([row[:, None] * inv[None, :], col[:, None] * inv[None, :]], axis=-1)
        cs, sn = np.cos(ang).astype(np.float32), np.sin(ang).astype(np.float32)
        idx = np.arange(dim) % half
        return np.ascontiguousarray(cs[:, idx].T), np.ascontiguousarray(sn[:, idx].T)

    ca, sa = tables(128)
    cb, sb_ = tables(64)
    cb = np.concatenate([cb, cb], axis=0)
    sb_ = np.concatenate([sb_, sb_], axis=0)
    return ca, sa, cb, sb_


def _rot_mats():
    ra = np.zeros((128, 128), np.float32)
    for d in range(64):
        ra[d + 64, d] = -1.0
        ra[d, d + 64] = 1.0
    rb = np.zeros((128, 128), np.float32)
    for blk in range(2):
        o = blk * 64
        for e in range(32):
            rb[o + e + 32, o + e] = -1.0
            rb[o + e, o + e + 32] = 1.0
    return ra, rb


def _masks(prompt):
    mA = np.zeros((9, NKEY), np.float32)
    mB = np.zeros((9, T), np.float32)
    if prompt:
        mA[8, 0:256] = 1.0
        for s in range(8):
            mA[s, 256 + s * 256:256 + (s + 1) * 256] = 1.0
        mB[:, :] = NEG
        for s in range(8):
            mB[s, s * 256:(s + 1) * 256] = 0.0
    return mA, mB


def _permute_w_in(w_in):
    qa = w_in[:, :, 0:1024]
    ka = w_in[:, :, 1024:1280]
    va = w_in[:, :, 1280:1536]
    nl_ = w_in.shape[0]
    qb = w_in[:, :, 1536:3072].reshape(nl_, D, 8, 192)
    qbn = qb[:, :, :, 0:128].reshape(nl_, D, 1024)
    qbr = qb[:, :, :, 128:192].reshape(nl_, D, 512)
    ckv = w_in[:, :, 3072:3328]
    kr = w_in[:, :, 3328:3392]
    return np.ascontiguousarray(np.concatenate([qa, qbn, qbr, ka, ckv, kr, kr, va], axis=2))


_NC_CACHE = {}


def _get_nc(n_layers=L):
    if n_layers not in _NC_CACHE:
        _NC_CACHE[n_layers] = MK(n_layers).build()
    return _NC_CACHE[n_layers]


def make_in_maps(inp, nl=L):
    f = lambda a: np.ascontiguousarray(np.asarray(a, dtype=np.float32))
    fl = lambda a: np.ascontiguousarray(np.asarray(a, dtype=np.float32)[:nl])
    shared = {
        "w_ada": fl(inp["w_ada"]), "b_ada": fl(inp["b_ada"]), "norm_attn": fl(inp["norm_attn"]),
        "norm_ffn": fl(inp["norm_ffn"]), "w_in": _permute_w_in(fl(inp["w_in"])), "qnorm": fl(inp["qnorm_a"]),
        "knorm": fl(inp["knorm_a"]), "kvnorm": fl(inp["kvnorm_b"]), "w_uk": fl(inp["w_uk_b"]), "w_uv": fl(inp["w_uv_b"]),
        "w_o": fl(inp["w_o"]), "w_gate": fl(inp["w_gate"]), "w_up": fl(inp["w_up"]), "w_down": fl(inp["w_down"]),
        "norm_final": f(inp["norm_final"]), "ident": np.eye(128, dtype=np.float32),
    }
    ra, rb = _rot_mats()
    shared["rotA"], shared["rotB"] = ra, rb
    xp = f(inp["x_prompt"])
    xsm = f(inp["x_sample"])
    ck, cv = f(inp["cache_k_a"]), f(inp["cache_v_a"])
    cc, cr = f(inp["cache_ckv_b"]), f(inp["cache_krope_b"])
    cvec_s, cvec_p = f(inp["c"]), f(inp["c_ctx"])
    tabs = {False: _rope_tables(False), True: _rope_tables(True)}
    msk = {False: _masks(False), True: _masks(True)}
    in_maps = []
    for core in range(8):
        prompt = core >= 4
        m = dict(shared)
        if prompt:
            i = core - 4
            m["x"] = np.ascontiguousarray(xp[i * 8:(i + 1) * 8].reshape(T, D))
            m["cvec"] = cvec_p
            m["cache_k"] = np.zeros((nl, 256, 256), np.float32)
            m["cache_v"] = np.zeros((nl, 256, 256), np.float32)
            m["cache_c"] = np.zeros((nl, 256, 256), np.float32)
            m["cache_r"] = np.zeros((nl, 256, 64), np.float32)
        else:
            b = core
            m["x"] = np.ascontiguousarray(xsm[b])
            m["cvec"] = np.ascontiguousarray(cvec_s[b])
            m["cache_k"] = np.ascontiguousarray(ck[b].reshape(L, 256, 256)[:nl])
            m["cache_v"] = np.ascontiguousarray(cv[b].reshape(L, 256, 256)[:nl])
            m["cache_c"] = np.ascontiguousarray(cc[b][:nl])
            m["cache_r"] = np.ascontiguousarray(cr[b][:nl])
        ca, sa, cb, sb_ = tabs[prompt]
        m["cosA"], m["sinA"], m["cosB"], m["sinB"] = ca, sa, cb, sb_
        m["maskA"], m["maskB"] = msk[prompt]
        in_maps.append(m)
    return in_maps


def assemble(results):
    y_sample = np.stack([results[b]["y"] for b in range(4)], axis=0)
    y_prompt = np.concatenate([results[4 + i]["y"].reshape(8, 256, D) for i in range(4)], axis=0)

    def gather(name, tail):
        parts = []
        for i in range(4):
            a = results[4 + i][name]
            a = a.reshape(L, 8, 256, *tail).transpose(1, 0, 2, *range(3, 3 + len(tail)))
            parts.append(a)
        return np.ascontiguousarray(np.concatenate(parts, axis=0))

    new_k = gather("onk", (2, 128))
    new_v = gather("onv", (2, 128))
    new_c = gather("onc", (256,))
    new_r = gather("onr", (64,))
    return (y_prompt.astype(np.float32), y_sample.astype(np.float32), new_k, new_v, new_c, new_r)


def kernel(**inputs):
    nc = _get_nc(L)
    in_maps = make_in_maps(inputs)
    res = run_bass_kernel_spmd(nc, in_maps, core_ids=list(range(8)))
    return assemble(res.results)
```
